# Optimizing a Trainium2 kernel written in Bass

```python
import jax, jax.numpy as jnp
from jax import lax
import numpy as np

D_MODEL = 2048
BATCH = 2
SEQ = 8192
DEPTH = 2

GRID_W = 64
CTX_LEN = 256
N_HEADS = 32
N_KV_HEADS = 4
HEAD_DIM = 64
GROUPS = N_HEADS // N_KV_HEADS
WINDOW = 128
BLOCK = 128
ROPE_THETA = 10000.0
D_CONV = D_MODEL
CONV_WIDTH = 3
D_FF = ((8 * D_MODEL // 3 + 255) // 256) * 256
Q_W = N_HEADS * HEAD_DIM
KV_W = N_KV_HEADS * HEAD_DIM
IN_SIZES = (Q_W, KV_W, KV_W, D_CONV, D_CONV, D_CONV, D_MODEL, D_MODEL)
IN_COLS = sum(IN_SIZES)
IN_OFFSETS = tuple(int(o) for o in np.cumsum(IN_SIZES)[:-1])
EPS = 1e-6
NEG = -1e30

kernel_name = "hybrid_conv_swa_dit_block"


def rmsnorm(x, g):
    xf = x.astype(jnp.float32)
    y = xf * lax.rsqrt(jnp.mean(xf * xf, axis=-1, keepdims=True) + EPS)
    return y.astype(x.dtype) * g


def modulate(x, g, shift, scale):
    return rmsnorm(x, g) * (1.0 + scale) + shift


def ada_split(cvec, w, b):
    m = jax.nn.silu(cvec) @ w + b
    return jnp.split(m, 6, axis=-1)


def rope_1d(x, pos):
    half = x.shape[-1] // 2
    freqs = ROPE_THETA ** (-jnp.arange(half, dtype=jnp.float32) / half)
    ang = pos[:, None] * freqs[None, :]
    cos = jnp.cos(ang)[None, :, None, :].astype(x.dtype)
    sin = jnp.sin(ang)[None, :, None, :].astype(x.dtype)
    x1, x2 = x[..., :half], x[..., half:]
    return jnp.concatenate([x1 * cos - x2 * sin, x2 * cos + x1 * sin], axis=-1)


def axial_rope(x, row, col):
    d = x.shape[-1] // 2
    return jnp.concatenate([rope_1d(x[..., :d], row), rope_1d(x[..., d:], col)], axis=-1)


def softmax_with_sink(logits, sink):
    s = jnp.broadcast_to(sink.astype(jnp.float32).reshape(N_KV_HEADS, GROUPS)[None, :, :, None, None],
                         logits.shape[:-1] + (1,))
    p = jax.nn.softmax(jnp.concatenate([logits, s], axis=-1), axis=-1)
    return p[..., :-1]


def context_attention(q, k, v, sink):
    b, n = q.shape[0], q.shape[1]
    qg = q.reshape(b, n, N_KV_HEADS, GROUPS, HEAD_DIM)
    s = jnp.einsum('bqkgd,bskd->bkgqs', qg, k).astype(jnp.float32) * (HEAD_DIM ** -0.5)
    p = softmax_with_sink(s, sink).astype(v.dtype)
    o = jnp.einsum('bkgqs,bskd->bqkgd', p, v)
    return o.reshape(b, n, Q_W)


def latent_window_attention(q, k, v, kc, vc, sink):
    b, s_len = q.shape[0], q.shape[1]
    nb = s_len // BLOCK
    qb = q.reshape(b, nb, BLOCK, N_KV_HEADS, GROUPS, HEAD_DIM).transpose(1, 0, 2, 3, 4, 5)

    def band(t):
        tp = jnp.pad(t, ((0, 0), (BLOCK, BLOCK), (0, 0), (0, 0)))
        tp = tp.reshape(b, nb + 2, BLOCK, N_KV_HEADS, HEAD_DIM)
        tb = jnp.concatenate([tp[:, :-2], tp[:, 1:-1], tp[:, 2:]], axis=2)
        return tb.transpose(1, 0, 2, 3, 4)

    kb, vb = band(k), band(v)
    bids = jnp.arange(nb, dtype=jnp.int32)
    scale = HEAD_DIM ** -0.5

    def one_block(args):
        q_blk, k_blk, v_blk, bid = args
        s_loc = jnp.einsum('bqkgd,bskd->bkgqs', q_blk, k_blk).astype(jnp.float32) * scale
        s_ctx = jnp.einsum('bqkgd,bskd->bkgqs', q_blk, kc).astype(jnp.float32) * scale
        qpos = bid * BLOCK + jnp.arange(BLOCK, dtype=jnp.int32)
        kpos = (bid - 1) * BLOCK + jnp.arange(3 * BLOCK, dtype=jnp.int32)
        valid = (jnp.abs(kpos[None, :] - qpos[:, None]) <= WINDOW) & (kpos[None, :] >= 0) & (kpos[None, :] < s_len)
        s_loc = jnp.where(valid[None, None, None], s_loc, NEG)
        p = softmax_with_sink(jnp.concatenate([s_ctx, s_loc], axis=-1), sink).astype(v_blk.dtype)
        n_ctx = kc.shape[1]
        o = jnp.einsum('bkgqs,bskd->bqkgd', p[..., :n_ctx], vc) + \
            jnp.einsum('bkgqs,bskd->bqkgd', p[..., n_ctx:], v_blk)
        return o

    o = lax.map(one_block, (qb, kb, vb, bids))
    return o.transpose(1, 0, 2, 3, 4, 5).reshape(b, s_len, Q_W)


def short_conv(u, w, bias):
    up = jnp.pad(u, ((0, 0), (1, 1), (0, 0)))
    return up[:, :-2] * w[0] + up[:, 1:-1] * w[1] + up[:, 2:] * w[2] + bias


def merge_branches(attn, cb, cc, cx, ga, gc, conv_w, conv_b, w_attn_out, w_conv_out, w_o):
    attn_branch = attn @ w_attn_out
    conv_branch = (cb * short_conv(cc * cx, conv_w, conv_b)) @ w_conv_out
    m = jax.nn.sigmoid(ga) * attn_branch + jax.nn.sigmoid(gc) * conv_branch
    return m @ w_o


def swiglu(h, w_in, w_out):
    gate, up = jnp.split(h @ w_in, 2, axis=-1)
    return (jax.nn.silu(gate) * up) @ w_out


def heads(t, n):
    return t.reshape(t.shape[0], t.shape[1], n, HEAD_DIM)


def setup_inputs(seed: int = 0) -> dict:
    key = jax.random.key(seed)
    ks = jax.random.split(key, 20)
    f32 = jnp.float32

    def nrm(k, shape, scale):
        return jax.random.normal(k, shape, f32) * scale

    return {
        "x": nrm(ks[0], (BATCH, SEQ, D_MODEL), 1.0),
        "c": nrm(ks[1], (BATCH, D_MODEL), 1.0),
        "ctx": nrm(ks[2], (BATCH, CTX_LEN, D_MODEL), 1.0),
        "c_ctx": nrm(ks[3], (D_MODEL,), 1.0),
        "ada_w": nrm(ks[4], (DEPTH, D_MODEL, 6 * D_MODEL), 0.5 * D_MODEL ** -0.5),
        "ada_b": nrm(ks[5], (DEPTH, 6 * D_MODEL), 0.01),
        "norm1_g": 1.0 + nrm(ks[6], (DEPTH, D_MODEL), 0.01),
        "norm2_g": 1.0 + nrm(ks[7], (DEPTH, D_MODEL), 0.01),
        "w_in": nrm(ks[8], (DEPTH, D_MODEL, IN_COLS), D_MODEL ** -0.5),
        "conv_w": nrm(ks[9], (DEPTH, CONV_WIDTH, D_CONV), CONV_WIDTH ** -0.5),
        "conv_b": nrm(ks[10], (DEPTH, D_CONV), 0.01),
        "sink": nrm(ks[11], (DEPTH, N_HEADS), 0.5),
        "w_attn_out": nrm(ks[12], (DEPTH, Q_W, D_MODEL), Q_W ** -0.5),
        "w_conv_out": nrm(ks[13], (DEPTH, D_CONV, D_MODEL), D_CONV ** -0.5),
        "w_o": nrm(ks[14], (DEPTH, D_MODEL, D_MODEL), D_MODEL ** -0.5),
        "w_ffn_in": nrm(ks[15], (DEPTH, D_MODEL, 2 * D_FF), D_MODEL ** -0.5),
        "w_ffn_out": nrm(ks[16], (DEPTH, D_FF, D_MODEL), D_FF ** -0.5),
        "final_g": 1.0 + nrm(ks[17], (D_MODEL,), 0.01),
    }


def reference(x, c, ctx, c_ctx, ada_w, ada_b, norm1_g, norm2_g, w_in, conv_w, conv_b, sink,
              w_attn_out, w_conv_out, w_o, w_ffn_in, w_ffn_out, final_g):
    n_lat = x.shape[1]
    ROWS = n_lat // GRID_W
    grid_r, grid_c = jnp.meshgrid(jnp.arange(ROWS, dtype=jnp.int32), jnp.arange(GRID_W, dtype=jnp.int32), indexing='ij')
    row = grid_r.reshape(-1).astype(jnp.float32)
    col = grid_c.reshape(-1).astype(jnp.float32)

    xc = ctx
    for l in range(DEPTH):
        last = l == DEPTH - 1
        sh1, sc1, g1, sh2, sc2, g2 = [t[:, None, :] for t in ada_split(c, ada_w[l], ada_b[l])]
        csh1, csc1, cg1, csh2, csc2, cg2 = ada_split(c_ctx, ada_w[l], ada_b[l])

        h = modulate(x, norm1_g[l], sh1, sc1)
        q, k, v, cb, cc, cx, ga, gc = jnp.split(h @ w_in[l], IN_OFFSETS, axis=-1)
        q = axial_rope(heads(q, N_HEADS), row, col)
        k = axial_rope(heads(k, N_KV_HEADS), row, col)
        v = heads(v, N_KV_HEADS)

        hc = modulate(xc, norm1_g[l], csh1, csc1)
        if last:
            kc_, vc_ = jnp.split(hc @ w_in[l][:, Q_W:Q_W + 2 * KV_W], 2, axis=-1)
        else:
            qc_, kc_, vc_, cbc, ccc, cxc, gac, gcc = jnp.split(hc @ w_in[l], IN_OFFSETS, axis=-1)
        kc = heads(kc_, N_KV_HEADS)
        vc = heads(vc_, N_KV_HEADS)

        attn = latent_window_attention(q, k, v, kc, vc, sink[l])
        x = x + g1 * merge_branches(attn, cb, cc, cx, ga, gc, conv_w[l], conv_b[l],
                                    w_attn_out[l], w_conv_out[l], w_o[l])

        if not last:
            attn_c = context_attention(heads(qc_, N_HEADS), kc, vc, sink[l])
            xc = xc + cg1 * merge_branches(attn_c, cbc, ccc, cxc, gac, gcc, conv_w[l], conv_b[l],
                                           w_attn_out[l], w_conv_out[l], w_o[l])
            xc = xc + cg2 * swiglu(modulate(xc, norm2_g[l], csh2, csc2), w_ffn_in[l], w_ffn_out[l])

        x = x + g2 * swiglu(modulate(x, norm2_g[l], sh2, sc2), w_ffn_in[l], w_ffn_out[l])

    return rmsnorm(x, final_g)
```

```python
import contextlib
import numpy as np
import concourse.bass as bass
import concourse.mybir as mybir
from concourse.bass_utils import run_bass_kernel_spmd

F32 = mybir.dt.float32
BF16 = mybir.dt.bfloat16
AF = mybir.ActivationFunctionType
ALU = mybir.AluOpType

D = 2048
NCH = 16
DFF = 5632
NFF = 44
SEQ = 8192
CTX = 256
NTOK = 2560
GRID_W = 64
SLOT = 6144
NSLOT = 4
NT = 7
NB = 6
EPS = 1e-6
NEGB = -30000.0
N_MAIN_CHUNKS = 16 + 48 + 64 + 16 + 88
N_MAIN_LOADS = (N_MAIN_CHUNKS + 2) // 3
LD_K, LD_V, LD_MAIN0 = 0, 1, 2
LD_FO0 = LD_MAIN0 + N_MAIN_LOADS
N_LOADS = LD_FO0 + 16
VC_ADAB, VC_N1G, VC_N2G, VC_CW, VC_CB, VC_SINK, VC_FG, NVEC = 0, 96, 112, 128, 176, 192, 208, 224
TB_COS, TB_SIN, TB_VAL = 0, NTOK, 2 * NTOK
TB_KB = 3 * NTOK
TB_TRI = TB_KB + 20
TB_PERM = TB_TRI + 1024
TB_ID = TB_PERM + 128
NTAB = TB_ID + 128


def _qperm():
    idx = np.zeros(D, np.int64)
    for c in range(16):
        m, i = divmod(c, 8)
        for p in range(128):
            head = 16 * m + i + 8 * (p // 64)
            idx[c * 128 + p] = head * 64 + (p % 64)
    return idx


def _tile_cols(W):
    K, n = W.shape
    return W.reshape(K // 128, 128, n).transpose(1, 0, 2)


def _build_wstream(l, w_in, w_attn_out, w_conv_out, w_o, w_ffn_in, w_ffn_out):
    qp = _qperm()
    wi = w_in[l]
    ws = np.zeros((N_LOADS, 128, SLOT), np.float32)
    q = wi[:, 0:2048][:, qp]
    k = wi[:, 2048:2304]
    v = wi[:, 2304:2560]
    cb, cc, cx = wi[:, 2560:4608], wi[:, 4608:6656], wi[:, 6656:8704]
    ga, gc = wi[:, 8704:10752], wi[:, 10752:12800]
    ao = w_attn_out[l][qp, :]
    co = w_conv_out[l]
    wo = w_o[l]
    fi = w_ffn_in[l]
    fo = w_ffn_out[l]

    def chunk(W, j):
        return _tile_cols(W[:, 128 * j:128 * j + 128])

    ws[LD_K, :, 0:4096] = np.stack([chunk(k, 0), chunk(k, 1)], 1).reshape(128, -1)
    ws[LD_V, :, 0:4096] = _tile_cols(v).reshape(128, -1)
    chunks = []
    for j in range(16):
        chunks.append(chunk(q, j))
    for j in range(16):
        chunks += [chunk(cx, j), chunk(cc, j), chunk(cb, j)]
    for j in range(16):
        chunks += [chunk(ga, j), chunk(gc, j), chunk(ao, j), chunk(co, j)]
    for j in range(16):
        chunks.append(chunk(wo, j))
    for j in range(NFF):
        chunks += [chunk(fi[:, 0:DFF], j), chunk(fi[:, DFF:], j)]
    assert len(chunks) == N_MAIN_CHUNKS
    for i in range(N_MAIN_LOADS):
        grp = chunks[3 * i:3 * i + 3]
        ws[LD_MAIN0 + i, :, 0:len(grp) * 2048] = np.stack(grp, 1).reshape(128, -1)
    for j in range(16):
        ws[LD_FO0 + j, :, 0:NFF * 128] = _tile_cols(fo[:, 128 * j:128 * j + 128]).reshape(128, -1)
    return ws


def _build_wada(l, ada_w):
    wa = np.zeros((32, 128, SLOT), np.float32)
    for i in range(32):
        grp = [_tile_cols(ada_w[l][:, 128 * (3 * i + t):128 * (3 * i + t) + 128]) for t in range(3)]
        wa[i] = np.stack(grp, 1).reshape(128, -1)
    return wa


def _fm(vec):
    return vec.reshape(-1, 128).T


def _build_vec(l, ada_b, norm1_g, norm2_g, conv_w, conv_b, sink, final_g):
    v = np.zeros((128, NVEC), np.float32)
    v[:, VC_ADAB:VC_ADAB + 96] = _fm(ada_b[l])
    v[:, VC_N1G:VC_N1G + 16] = _fm(norm1_g[l])
    v[:, VC_N2G:VC_N2G + 16] = _fm(norm2_g[l])
    cw = np.stack([_fm(conv_w[l][t]) for t in range(3)], -1)
    v[:, VC_CW:VC_CW + 48] = cw.reshape(128, 48)
    v[:, VC_CB:VC_CB + 16] = _fm(conv_b[l])
    qp = _qperm()
    v[:, VC_SINK:VC_SINK + 16] = _fm(sink[l][qp // 64])
    v[:, VC_FG:VC_FG + 16] = _fm(final_g)
    return v


def _build_tab(base):
    tab = np.zeros((128, NTAB), np.float32)
    t = base + np.arange(NTOK)
    row = (t // GRID_W).astype(np.float32)
    col = (t % GRID_W).astype(np.float32)
    freqs = (np.float32(10000.0) ** (-np.arange(16, dtype=np.float32) / np.float32(16))).astype(np.float32)
    for p in range(128):
        d = p % 64
        e = d % 32
        pos = row if d < 32 else col
        ang = (pos * freqs[e % 16]).astype(np.float32)
        tab[p, TB_COS:TB_COS + NTOK] = np.cos(ang)
        s = np.sin(ang)
        tab[p, TB_SIN:TB_SIN + NTOK] = -s if e < 16 else s
        pe = e + 16 if e < 16 else e - 16
        tab[p - e + pe, TB_PERM + p] = 1.0
        tab[p, TB_ID + p] = 1.0
    valid = ((t >= 0) & (t < SEQ)).astype(np.float32)
    tab[:, TB_VAL:TB_VAL + NTOK] = valid[None, :]
    tab[:, TB_KB:TB_KB + 20] = np.where(valid.reshape(20, 128).T > 0, 0.0, NEGB)
    jj = np.arange(128)[:, None]
    ii = np.arange(128)[None, :]
    tab[:, TB_TRI:TB_TRI + 512] = np.tile((jj >= ii).astype(np.float32), (1, 4))
    tab[:, TB_TRI + 512:TB_TRI + 1024] = np.tile((jj <= ii).astype(np.float32), (1, 4))
    return tab


class Sem:
    def __init__(self, h, name):
        self.h = h
        self.name = name
        self.count = 0


class Eng:
    def __init__(self, name, sem):
        self.name = name
        self.sem = sem
        self.ops = []
        self.waited = {}


class Prog:
    def __init__(self, nc, es):
        self.nc = nc
        self.es = es
        self.engs = {}
        for n in ["pe", "act", "dve", "pool", "sp"]:
            self.engs[n] = Eng(n, self.new_sem("c_" + n))
        self.lastw = {}
        self.readers = {}
        self.ps_ptr = {"all": 0, "A": 0, "B": 0, "C": 0, "six": 0}
        self.t_ptr = 0
        self.b_ptr = 0

    def new_sem(self, name):
        return Sem(self.es.enter_context(self.nc.semaphore(name)), name)

    def sbuf(self, name, shape, dt):
        return self.es.enter_context(self.nc.sbuf_tensor(name, shape, dt))

    def op(self, eng, reads, writes, fn, dsem=None):
        E = self.engs[eng]
        deps = {}
        psr = [k for k in reads if k[0] == "ps"]
        if psr:
            reads = [k for k in reads if k[0] != "ps"]
            writes = list(writes) + [k for k in psr if k not in writes]

        def add(sv):
            if sv is not None:
                s, v = sv
                if deps.get(s, 0) < v:
                    deps[s] = v
        for k in reads:
            add(self.lastw.get(k))
        for k in writes:
            add(self.lastw.get(k))
            for s, v in self.readers.get(k, {}).items():
                add((s, v))
        waits = []
        for s, v in deps.items():
            if s is E.sem and eng == "pe":
                continue
            if E.waited.get(s, 0) >= v:
                continue
            E.waited[s] = v
            waits.append((s, v))
        if dsem is None:
            E.sem.count += 1
            done = (E.sem, E.sem.count)
            inc = (E.sem, 1)
        else:
            dsem.count += 16
            done = (dsem, dsem.count)
            inc = (dsem, 16)
        E.ops.append((waits, fn, inc))
        for k in reads:
            r = self.readers.setdefault(k, {})
            if r.get(done[0], 0) < done[1]:
                r[done[0]] = done[1]
        for k in writes:
            self.lastw[k] = done
            self.readers[k] = {}
        return done

    def wait_all(self, eng, sems):
        E = self.engs[eng]
        waits = [(s, s.count) for s in sems if s.count > 0]
        E.ops.append((waits, None, None))

    def ps(self, n=1, pool="all"):
        lo, size = {"all": (0, 8), "A": (0, 4), "B": (4, 2), "C": (6, 2), "six": (0, 6)}[pool]
        p = self.ps_ptr[pool]
        if n == 2 and p % 2 == 1:
            p += 1
        if p + n > size:
            p = 0
        self.ps_ptr[pool] = (p + n) % size
        return lo + p

    def tmp(self):
        i = self.t_ptr
        self.t_ptr = (i + 1) % NT
        return i

    def btmp(self):
        i = self.b_ptr
        self.b_ptr = (i + 1) % NB
        return i

    def bpair(self):
        i = self.b_ptr
        if i % 2 == 1:
            i += 1
        if i + 2 > NB:
            i = 0
        self.b_ptr = (i + 2) % NB
        return i

    def emit(self, block):
        def runner(E):
            def run(e):
                for waits, fn, inc in E.ops:
                    for s, v in waits:
                        e.wait_ge(s.h, v)
                    if fn is not None:
                        ins = fn(e)
                        ins.then_inc(inc[0].h, inc[1])
            return run
        block.tensor(runner(self.engs["pe"]))
        block.scalar(runner(self.engs["act"]))
        block.vector(runner(self.engs["dve"]))
        block.gpsimd(runner(self.engs["pool"]))
        block.sync(runner(self.engs["sp"]))


def _halves(n):
    if n <= 512:
        return 1, n
    assert n % 2 == 0
    return 2, n // 2


class StopBuild(Exception):
    pass


class Kern:
    def __init__(self, layers=(0, 1), debug=None, stop=None):
        self.layers = layers
        self.debug = debug or {}
        self.stop = stop
        self.nstage = 0
        nc = bass.Bass("TRN2", target_bir_lowering=False)
        self.nc = nc
        self.es = contextlib.ExitStack()
        with self.es:
            self.build()

    def build(self):
        nc = self.nc
        P = Prog(nc, self.es)
        self.P = P
        dt = nc.dram_tensor
        self.xT = dt("xT", [D, NTOK], F32, kind="ExternalInput").ap()
        self.ctxT = dt("ctxT", [D, CTX + 2], F32, kind="ExternalInput").ap()
        self.cvec = dt("cvec", [128, 32], F32, kind="ExternalInput").ap()
        self.tab = dt("tab", [128, NTAB], F32, kind="ExternalInput").ap()
        self.vec = [dt("vec%d" % l, [128, NVEC], F32, kind="ExternalInput").ap() for l in range(2)]
        self.ws = [dt("ws%d" % l, [N_LOADS, 128, SLOT], F32, kind="ExternalInput").ap() for l in range(2)]
        self.wa = [dt("wa%d" % l, [32, 128, SLOT], F32, kind="ExternalInput").ap() for l in range(2)]
        self.outT = dt("outT", [D, 2048], F32, kind="ExternalOutput").ap()
        self.x1T = dt("x1T", [D, NTOK], F32, kind="Internal").ap()
        self.xcT = dt("xcT", [D, CTX + 2], F32, kind="Internal").ap()
        self.wsb = dt("wsb", [N_LOADS, 128, SLOT], BF16, kind="Internal").ap()
        self.wb_tile = None
        self.use_bf16 = False
        self.dbg_out = {}
        self.dbg_sems = []

        sb = P.sbuf
        self.xt = sb("xt", [128, 16, 514], F32)
        self.hT = sb("hT", [128, 16, 514], BF16)
        self.big = sb("big", [128, 24576], BF16)
        self.wsl = [sb("wsl%d" % i, [128, 48, 128], BF16) for i in range(NSLOT)]
        self.Ksb = sb("Ksb", [128, 2, NTOK], BF16)
        self.Vsb = sb("Vsb", [128, 20, 256], BF16)
        self.Kc = sb("Kc", [128, 2, CTX], BF16)
        self.Vc = sb("Vc", [128, 2, 256], BF16)
        self.cosb = sb("cosb", [128, 512], F32)
        self.sinb = sb("sinb", [128, 512], F32)
        self.valb = sb("valb", [128, 514], F32)
        self.Tr = sb("Tr", [128, NT, 514], F32)
        self.rstd = sb("rstd", [128, 514], F32)
        self.Br = sb("Br", [128, NB, 514], BF16)
        self.sexp = sb("sexp", [128, 16], F32)
        self.trif = sb("trif", [128, 2, 512], F32)
        self.maskb = sb("maskb", [128, 2, 512], BF16)
        self.identf = sb("identf", [128, 128], F32)
        self.ident = sb("ident", [128, 128], BF16)
        self.kbias = sb("kbias", [128, 20], F32)
        self.permf = sb("permf", [128, 128], F32)
        self.perm = sb("perm", [128, 128], BF16)
        self.ones64 = sb("ones64", [128, 64], BF16)
        self.onesK = sb("onesK", [128, 128], BF16)
        self.epst = sb("epst", [128, 1], F32)
        self.vecs = [sb("vecs%d" % l, [128, NVEC], F32) for l in range(2)]
        self.cv = sb("cv", [128, 32], F32)
        self.cs = sb("cs", [128, 32], BF16)
        self.mod = sb("mod", [128, 2, 96], F32)
        self.modraw = sb("modraw", [128, 2, 96, 2], F32)
        self.ada_queue = [(0, i) for i in range(32)]
        self.ada_todo = {}
        self.allow_ada = False
        self.ada_period = 3
        self.n_wloads = 0
        self.gs = sb("gs", [128, 2, 2, 16], F32)
        self.PS = self.es.enter_context(nc.psum_tensor("PS", [128, 8, 512], F32))
        self.s_w = [P.new_sem("w%d" % i) for i in range(NSLOT)]
        self.s_wb = [P.new_sem("wb%d" % i) for i in range(NSLOT)]
        self.s_ldx = [P.new_sem("ldx%d" % i) for i in range(4)]
        self.s_ldx2 = [P.new_sem("ldy%d" % i) for i in range(4)]
        self.s_cos = P.new_sem("ldcos")
        self.s_sin = P.new_sem("ldsin")
        self.s_val = P.new_sem("ldval")
        self.s_st = [P.new_sem("st%d" % i) for i in range(4)]
        self.s_c = P.new_sem("cst")
        self.w_ptr = 0
        self.ring = "all"
        self.pinned = set()
        self.kv_slots = None
        self.qy = self.big[:, 0:8192].rearrange("p (c t) -> p c t", t=512)
        self.at = self.big[:, 8192:16384].rearrange("p (c t) -> p c t", t=512)
        self.yT = self.big[:, 16384:24576].rearrange("p (c t) -> p c t", t=512)
        self.mT = self.qy
        self.actT = self.big[:, 0:NFF * 512].rearrange("p (c t) -> p c t", t=512)
        self.xt2 = self.big[:, 0:2 * 16 * 514].bitcast(F32).rearrange("p (c t) -> p c t", t=514)

        try:
            self.consts()
            for l in self.layers:
                self.layer(l)
        except StopBuild:
            pass
        P.wait_all("sp", self.s_st + self.dbg_sems)
        with nc.Block() as block:
            P.emit(block)

    def dump(self, tag, name, ap, shape, dtype, reads):
        if tag not in self.debug:
            return
        full = "dbg_%s_%s" % (tag, name)
        d = self.nc.dram_tensor(full, list(shape), dtype, kind="ExternalOutput").ap()
        self.dbg_out[full] = d
        sem = self.P.new_sem("s_" + full)
        self.dma("sp", d, ap, reads, [], sem)
        self.dbg_sems.append(sem)

    def stage(self, name):
        self.nstage += 1
        if self.stop is not None and name == self.stop:
            raise StopBuild()

    def dma(self, q, out, in_, reads, writes, dsem):
        return self.P.op(q, reads, writes, lambda e, o=out, i=in_: e.dma_start(out=o, in_=i), dsem=dsem)

    def next_slot(self):
        s = self.w_ptr
        while s in self.pinned:
            s = (s + 1) % NSLOT
        self.w_ptr = (s + 1) % NSLOT
        return s

    def wload(self, src_ap, n):
        s = self.next_slot()
        flat = self.wsl[s][:].rearrange("p a b -> p (a b)")
        self.dma("pool", flat[:, 0:n], src_ap[:, 0:n], [], [("w", s)], self.s_w[s])
        return s

    def wload_main(self, l, lid, n):
        if l == 0 and (self.use_bf16 or (self.wb_tile is not None and lid % 4 < self.wb_tile)):
            s = self.next_slot()
            flat = self.wsl[s][:].rearrange("p a b -> p (a b)")
            self.dma("pool", flat[:, 0:n], self.wsb[lid][:, 0:n], [("wsb", lid)], [("w", s)], self.s_w[s])
            return s
        s = self.wload(self.ws[l][lid], n)
        if l == 0 and self.wb_tile is not None and lid % 4 == self.wb_tile:
            flat = self.wsl[s][:].rearrange("p a b -> p (a b)")
            self.dma("sp", self.wsb[lid][:, 0:n], flat[:, 0:n], [("w", s)], [("wsb", lid)], self.s_wb[s])
        return s

    def mm(self, mms, reads, writes):
        def fn(e, mms=mms):
            ins = None
            for (o, l, r, st, sp_) in mms:
                ins = e.matmul(o, lhsT=l, rhs=r, start=st, stop=sp_)
            return ins
        return self.P.op("pe", reads, writes, fn)

    def act(self, out, in_, func, reads, writes, bias=None, scale=None):
        kw = {}
        if bias is not None:
            kw["bias"] = bias
        if scale is not None:
            kw["scale"] = scale
        return self.P.op("act", reads, writes,
                         lambda e, o=out, i=in_, f=func, kw=kw: e.activation(out=o, in_=i, func=f, **kw))

    def tt(self, out, in0, in1, op, reads, writes, eng="dve"):
        return self.P.op(eng, reads, writes,
                         lambda e, o=out, a=in0, b=in1, op=op: e.tensor_tensor(out=o, in0=a, in1=b, op=op))

    def stt(self, out, in0, scalar, in1, op0, op1, reads, writes):
        return self.P.op("dve", reads, writes,
                         lambda e, o=out, a=in0, s=scalar, b=in1, o0=op0, o1=op1:
                         e.scalar_tensor_tensor(out=o, in0=a, scalar=s, in1=b, op0=o0, op1=o1))

    def psv(self, b, nh, hl, p0=0, p1=128):
        if nh == 1:
            return self.PS[p0:p1, b, 0:hl]
        return self.PS[p0:p1, b:b + nh, 0:hl]

    def v3(self, ap2d, nh):
        if nh == 1:
            return ap2d
        return ap2d.rearrange("p (h l) -> p h l", h=nh)

    def pskeys(self, b, nh):
        return [("ps", b + i) for i in range(nh)]

    def consts(self):
        P = self.P
        sp = "sp"
        self.dma(sp, self.trif[:].rearrange("p a b -> p (a b)"), self.tab[:, TB_TRI:TB_TRI + 1024], [], [("trif",)], self.s_c)
        self.dma(sp, self.identf[:], self.tab[:, TB_ID:TB_ID + 128], [], [("identf",)], self.s_c)
        self.dma(sp, self.kbias[:], self.tab[:, TB_KB:TB_KB + 20], [], [("kbias",)], self.s_c)
        self.dma(sp, self.permf[:], self.tab[:, TB_PERM:TB_PERM + 128], [], [("permf",)], self.s_c)
        self.dma(sp, self.cv[:], self.cvec[:, :], [], [("cv",)], self.s_c)
        for l in range(2):
            self.dma(sp, self.vecs[l][:], self.vec[l][:, :], [], [("vecs", l)], self.s_c)
        for key in [("trif",), ("identf",), ("kbias",), ("permf",), ("cv",), ("vecs", 0), ("vecs", 1)]:
            P.lastw[key] = (self.s_c, self.s_c.count)
        P.op("dve", [("permf",)], [("perm",)], lambda e: e.tensor_copy(out=self.perm[:], in_=self.permf[:]))
        P.op("dve", [("identf",)], [("ident",)], lambda e: e.tensor_copy(out=self.ident[:], in_=self.identf[:]))
        P.op("dve", [("trif",)], [("maskb",)],
             lambda e: e.tensor_scalar(out=self.maskb[:], in0=self.trif[:], scalar1=1.0e6, scalar2=-1.0e6,
                                       op0=ALU.mult, op1=ALU.add))
        P.op("dve", [], [("ones64",)], lambda e: e.memset(self.ones64[:], 1.0))
        P.op("dve", [], [("onesK",)], lambda e: e.memset(self.onesK[:], 1.0 / D))
        P.op("dve", [], [("eps",)], lambda e: e.memset(self.epst[:], EPS))
        self.act(self.cs[:], self.cv[:], AF.Silu, [("cv",)], [("cs",)])

    class Stream:
        def __init__(self, K, l, first_load):
            self.K, self.l, self.load, self.ci, self.slot = K, l, first_load, 3, None

        def next(self):
            if self.ci == 3:
                K = self.K
                K.n_wloads += 1
                if K.allow_ada and K.ada_queue and K.n_wloads % K.ada_period == 0:
                    K.ada_step(*K.ada_queue.pop(0))
                self.slot = K.wload_main(self.l, self.load, SLOT)
                self.load += 1
                self.ci = 0
            ci = self.ci
            self.ci += 1
            return self.slot, ci

    def ada_step(self, l, i):
        P = self.P
        b = P.ps(1, self.ring)
        psa = self.PS[:, b, 0:6].rearrange("p (c t) -> p c t", t=2)
        s = self.wload(self.wa[l][i], SLOT)
        for t in range(3):
            mms = [(psa[:, t, :], self.wsl[s][:, t * 16 + k, :], self.cs[:, 2 * k:2 * k + 2], k == 0, k == 15)
                   for k in range(16)]
            self.mm(mms, [("w", s), ("cs",)], [("ps", b)])
        P.op("dve", [("ps", b)], [("modraw", l)],
             lambda e, o=self.modraw[:, l, 3 * i:3 * i + 3, :], i_=psa: e.tensor_copy(out=o, in_=i_))

    def ada(self, l):
        self.stage("ada")
        while self.ada_queue and (self.ada_queue[0][0] != l or self.ada_queue[0][1] <= 10):
            self.ada_step(*self.ada_queue.pop(0))
        self.ada_fin(l, 0, 32)
        V = self.vecs[l]
        self.act(self.sexp[:], V[:, VC_SINK:VC_SINK + 16], AF.Exp, [("vecs", l)], [("se",)])
        self.ada_todo = {"wo": (15, 32, 48), "norm2": (26, 48, 80), "ffo": (31, 80, 96)}

    def ada_need(self, l, key):
        if key not in self.ada_todo:
            return
        step, lo, hi = self.ada_todo.pop(key)
        while self.ada_queue and self.ada_queue[0][0] == l and self.ada_queue[0][1] <= step:
            self.ada_step(*self.ada_queue.pop(0))
        self.ada_fin(l, lo, hi)

    def ada_fin(self, l, lo, hi):
        V = self.vecs[l]
        for t in range(2):
            self.tt(self.mod[:, t, lo:hi], self.modraw[:, l, lo:hi, t], V[:, VC_ADAB + lo:VC_ADAB + hi], ALU.add,
                    [("modraw", l), ("vecs", l)], [("mod",)])
        for t in range(2):
            if lo <= 16 and hi >= 32:
                self.stt(self.gs[:, t, 0, :], self.mod[:, t, 16:32], 1.0, V[:, VC_N1G:VC_N1G + 16], ALU.add, ALU.mult,
                         [("mod",), ("vecs", l)], [("gs",)])
            if lo <= 64 and hi >= 80:
                self.stt(self.gs[:, t, 1, :], self.mod[:, t, 64:80], 1.0, V[:, VC_N2G:VC_N2G + 16], ALU.add, ALU.mult,
                         [("mod",), ("vecs", l)], [("gs",)])

    def norm_begin(self, n, pool="all"):
        nh, hl = _halves(n)
        b = self.P.ps(nh, pool)
        return {"n": n, "nh": nh, "hl": hl, "b": b}

    def norm_stat(self, ns, c, c0, xb=None):
        P = self.P
        xbuf, xkey = xb or (self.xt, "xt")
        n, nh, hl, b = ns["n"], ns["nh"], ns["hl"], ns["b"]
        bi = P.btmp()
        sq = self.Br[:, bi, 0:n]
        self.act(sq, xbuf[:, c, c0:c0 + n], AF.Square, [(xkey, c)], [("B", bi)])
        mms = [(self.PS[:, b + h, 0:hl], self.onesK[:], self.Br[:, bi, h * hl:(h + 1) * hl], c == 0, c == 15)
               for h in range(nh)]
        self.mm(mms, [("B", bi), ("onesK",)], self.pskeys(b, nh))

    def norm_finish(self, ns, l, t, which, c0, final=False, xb=None):
        P = self.P
        xbuf, xkey = xb or (self.xt, "xt")
        n, nh, hl, b = ns["n"], ns["nh"], ns["hl"], ns["b"]
        pv = self.psv(b, nh, hl)
        t1 = P.tmp()
        self.act(self.v3(self.Tr[:, t1, 0:n], nh), pv, AF.Ln, self.pskeys(b, nh) + [("eps",)], [("T", t1)],
                 bias=self.epst[:, 0:1])
        rstd = self.rstd[:, 0:n]
        self.act(rstd, self.Tr[:, t1, 0:n], AF.Exp, [("T", t1)], [("rstd",)], scale=-0.5)
        V = self.vecs[l]
        for c in range(16):
            xs = xbuf[:, c, c0:c0 + n]
            if final:
                self.stt(xs, xs, V[:, VC_FG + c:VC_FG + c + 1], rstd, ALU.mult, ALU.mult,
                         [(xkey, c), ("rstd",), ("vecs", l)], [(xkey, c)])
                continue
            ti = P.tmp()
            self.stt(self.Tr[:, ti, 0:n], xs, self.gs[:, t, which, c:c + 1], rstd, ALU.mult, ALU.mult,
                     [(xkey, c), ("rstd",), ("gs",)], [("T", ti)])
            sh = self.mod[:, t, (0 if which == 0 else 48) + c:(0 if which == 0 else 48) + c + 1]
            self.act(self.hT[:, c, c0:c0 + n], self.Tr[:, ti, 0:n], AF.Identity, [("T", ti), ("mod",)], [("hT", c)],
                     bias=sh)

    def norm(self, l, t, which, c0, n, final=False, xb=None):
        ns = self.norm_begin(n)
        for c in range(16):
            self.norm_stat(ns, c, c0, xb)
        self.norm_finish(ns, l, t, which, c0, final, xb)

    def kv(self, l, n, rope, Kdst, Kkey, Vdst, Vkey, vblk0):
        P = self.P
        if self.kv_slots is not None:
            sk, sv = self.kv_slots
        else:
            sk = self.wload(self.ws[l][LD_K], 4096)
            sv = self.wload(self.ws[l][LD_V], 4096)
        wv = self.wsl[sv][:].rearrange("p a b -> p (a b)")[:, 0:4096].rearrange("p (k n) -> p k n", n=256)
        for m in range(2):
            b = P.ps(1, self.ring)
            mms = [(self.PS[:, b, 0:n], self.wsl[sk][:, m * 16 + k, :], self.hT[:, k, 1:1 + n], k == 0, k == 15)
                   for k in range(16)]
            self.mm(mms, [("w", sk)] + [("hT", k) for k in range(16)], [("ps", b)])
            if rope:
                self.stage("kv_rope")
                self.rope(b, n, Kdst(m), [Kkey])
                self.stage("kv_rope_done")
            else:
                self.act(Kdst(m), self.PS[:, b, 0:n], AF.Copy, [("ps", b)], [Kkey])
        for i in range(n // 128):
            b = P.ps(1, self.ring)
            mms = [(self.PS[:, b, 0:256], self.hT[:, k, 1 + 128 * i:1 + 128 * (i + 1)], wv[:, k, :], k == 0, k == 15)
                   for k in range(16)]
            self.mm(mms, [("w", sv)] + [("hT", k) for k in range(16)], [("ps", b)])
            self.act(Vdst(vblk0 + i), self.PS[:, b, 0:256], AF.Copy, [("ps", b)], [Vkey])

    def rope_a(self, b, n):
        P = self.P
        bi = P.btmp()
        qa = self.Br[:, bi, 0:n]
        self.act(qa, self.PS[:, b, 0:n], AF.Copy, [("ps", b)], [("B", bi)])
        t1 = P.tmp()
        self.tt(self.Tr[:, t1, 0:n], self.PS[:, b, 0:n], self.cosb[:, 0:n], ALU.mult, [("ps", b), ("cos_tab",)], [("T", t1)])
        return (bi, t1, n)

    def rope_b(self, state, dst, dkeys):
        P = self.P
        bi, t1, n = state
        qa = self.Br[:, bi, 0:n]
        b2 = P.ps(1, self.ring)
        self.mm([(self.PS[:, b2, 0:n], self.perm[:], qa, True, True)], [("B", bi), ("perm",)], [("ps", b2)])
        t2 = P.tmp()
        self.tt(self.Tr[:, t2, 0:n], self.PS[:, b2, 0:n], self.sinb[:, 0:n], ALU.mult, [("ps", b2), ("sin_tab",)], [("T", t2)])
        self.tt(dst, self.Tr[:, t1, 0:n], self.Tr[:, t2, 0:n], ALU.add, [("T", t1), ("T", t2)], dkeys)

    def rope(self, b, n, dst, dkeys):
        self.rope_b(self.rope_a(b, n), dst, dkeys)

    def load_tabs(self, a, n, halo):
        self.dma("sp", self.cosb[:, 0:n], self.tab[:, TB_COS + a:TB_COS + a + n], [], [("cos_tab",)], self.s_cos)
        self.dma("sp", self.sinb[:, 0:n], self.tab[:, TB_SIN + a:TB_SIN + a + n], [], [("sin_tab",)], self.s_sin)
        if halo:
            self.dma("sp", self.valb[:, 0:n + 2], self.tab[:, TB_VAL + a - 1:TB_VAL + a + n + 1], [], [("val",)], self.s_val)

    def load_x(self, src, a, n, halo, xb=None):
        xbuf, xkey = xb or (self.xt, "xt")
        sv = src.rearrange("(c p) t -> p c t", p=128)
        extra = []
        if xkey != "xt":
            for nm, cnt in (("qy", 16), ("at", 16), ("y", 16), ("m", 16), ("act", NFF)):
                extra += [(nm, c) for c in range(cnt)]
        for p in range(4):
            self.dma("sp", xbuf[:, 4 * p:4 * p + 4, 1 - halo:1 + n + halo], sv[:, 4 * p:4 * p + 4, a - halo:a + n + halo],
                     [("x1dram", q) for q in range(4)] if src is self.x1T or src is self.xcT else [],
                     [(xkey, c) for c in range(4 * p, 4 * p + 4)] + extra,
                     self.s_ldx[p] if xkey == "xt" else self.s_ldx2[p])

    def prepass_all(self, l, src, tiles):
        xb = lambda i: (self.xt2, "xt2") if i % 2 == 1 else None
        self.ring = "six"
        a0, n0 = tiles[0]
        self.load_x(src, a0, n0, 0, xb(0))
        ns = self.norm_begin(n0, "C")
        for c in range(16):
            self.norm_stat(ns, c, 1, xb(0))
        for i, (a, n) in enumerate(tiles):
            self.load_tabs(a, n, 0)
            nxt = tiles[i + 1] if i + 1 < len(tiles) else None
            if nxt:
                self.load_x(src, nxt[0], nxt[1], 0, xb(i + 1))
            self.norm_finish(ns, l, 0, 0, 1, xb=xb(i))
            if nxt:
                ns = self.norm_begin(nxt[1], "C")
                for c in range(16):
                    self.norm_stat(ns, c, 1, xb(i + 1))
            for _ in range(2):
                if self.ada_queue and self.ada_queue[0][0] == l:
                    self.ada_step(*self.ada_queue.pop(0))
            self.kv(l, n, True, lambda m, a=a, n=n: self.Ksb[:, m, a:a + n], ("K",),
                    lambda blk: self.Vsb[:, blk, :], ("V",), a // 128)
        self.ring = "all"

    def prepass(self, l, src, a, n, idx=0):
        xb = (self.xt2, "xt2") if idx % 2 == 1 else None
        self.load_x(src, a, n, 0, xb)
        self.load_tabs(a, n, 0)
        self.stage("pp_norm")
        self.norm(l, 0, 0, 1, n, xb=xb)
        for _ in range(5):
            if self.ada_queue and self.ada_queue[0][0] == l:
                self.ada_step(*self.ada_queue.pop(0))
        self.stage("pp_kv")
        self.kv(l, n, True, lambda m: self.Ksb[:, m, a:a + n], ("K",), lambda blk: self.Vsb[:, blk, :], ("V",), a // 128)

    def attention_gen(self, nblk, a, ctx_mode):
        P = self.P
        if ctx_mode:
            kcs = [("c", 0), ("c", 1)]
        else:
            kcs = [("c", 0), ("c", 1), ("l", 0), ("l", 1), ("l", 2)]
        steps = []
        for i in range(nblk):
            for quad in range(4):
                qs = {}
                for ki in range(len(kcs)):
                    steps.append({"i": i, "quad": quad, "ki": ki, "qs": qs})

        def qk(st):
            i, quad, (kind, kk) = st["i"], st["quad"], kcs[st["ki"]]
            ql, uq = 128 * i, a + 128 * i
            m, j = divmod(quad, 2)
            c0 = 8 * m + 4 * j
            bs = P.ps(2, "A")
            st["bs"] = bs
            masked = kind == "l" and kk != 1
            mms = []
            if masked:
                mb = self.maskb[:, 0 if kk == 0 else 1, :]
                for h in range(2):
                    mms.append((self.PS[:, bs + h, :], self.ident[:], mb, True, False))
            for h in range(2):
                if kind == "c":
                    kap = self.Kc[64 * h:64 * h + 64, m, 128 * kk:128 * kk + 128]
                    kkey = ("Kc",)
                else:
                    u0 = uq - 128 + 128 * kk
                    kap = self.Ksb[64 * h:64 * h + 64, m, u0:u0 + 128]
                    kkey = ("K",)
                qap = self.qy[64 * h:64 * h + 64, c0:c0 + 4, ql:ql + 128]
                mms.append((self.PS[:, bs + h, :], kap, qap, not masked, True))
            self.mm(mms, [kkey, ("ident",), ("maskb",)] + [("qy", c0 + t) for t in range(4)],
                    [("ps", bs), ("ps", bs + 1)])

        def ex(st):
            i, (kind, kk) = st["i"], kcs[st["ki"]]
            uq = a + 128 * i
            bs = st["bs"]
            bi = P.bpair()
            st["bi"] = bi
            pt2 = self.Br[:, bi:bi + 2, 0:512]
            rk = [("ps", bs), ("ps", bs + 1)]
            wk = [("B", bi), ("B", bi + 1)]
            if kind == "c":
                self.act(pt2, self.PS[:, bs:bs + 2, :], AF.Exp, rk, wk, scale=0.125)
            else:
                blk = (uq - 128 + 128 * kk) // 128
                self.act(pt2, self.PS[:, bs:bs + 2, :], AF.Exp, rk + [("kbias",)], wk,
                         bias=self.kbias[:, blk:blk + 1], scale=0.125)

        def pv(st):
            i, quad, ki, qs = st["i"], st["quad"], st["ki"], st["qs"]
            kind, kk = kcs[ki]
            uq = a + 128 * i
            m, j = divmod(quad, 2)
            if ki == 0:
                qs["bo"] = P.ps(1, "B")
                qs["bd"] = P.ps(1, "B")
            bo, bd = qs["bo"], qs["bd"]
            first, last = ki == 0, ki == len(kcs) - 1
            bi = st["bi"]
            rd = [("ones64",), ("B", bi), ("B", bi + 1)]
            vaps = []
            for h in range(2):
                g = 2 * m + h
                if kind == "c":
                    vaps.append(self.Vc[:, kk, 64 * g:64 * g + 64])
                    rd.append(("Vc",))
                else:
                    blk = (uq - 128 + 128 * kk) // 128
                    vaps.append(self.Vsb[:, blk, 64 * g:64 * g + 64])
                    rd.append(("V",))
            pts = [self.Br[:, bi + h, 0:512] for h in range(2)]
            mms = [(self.PS[0:64, bo, :], vaps[0], pts[0], first, last),
                   (self.PS[64:128, bo, :], vaps[1], pts[1], first, last),
                   (self.PS[0:64, bd, :], self.ones64[:, :], pts[0], first, last),
                   (self.PS[64:128, bd, :], self.ones64[:, :], pts[1], first, last)]
            self.mm(mms, rd, [("ps", bo), ("ps", bd)])
            if last:
                ql = 128 * i
                c0 = 8 * m + 4 * j
                t1 = P.tmp()
                for t in range(4):
                    self.act(self.Tr[:, t1, 128 * t:128 * t + 128], self.PS[:, bd, 128 * t:128 * t + 128], AF.Ln,
                             [("ps", bd), ("se",)], [("T", t1)], bias=self.sexp[:, c0 + t:c0 + t + 1])
                t2 = P.tmp()
                self.act(self.Tr[:, t2, 0:512], self.Tr[:, t1, 0:512], AF.Exp, [("T", t1)], [("T", t2)], scale=-1.0)
                ov = self.PS[:, bo, :].rearrange("p (c t) -> p c t", t=128)
                self.tt(self.at[:, c0:c0 + 4, ql:ql + 128], ov,
                        self.Tr[:, t2, 0:512].rearrange("p (c t) -> p c t", t=128),
                        ALU.mult, [("ps", bo), ("T", t2)], [("at", c0 + t) for t in range(4)])

        qk(steps[0])
        for idx, st in enumerate(steps):
            if idx + 1 < len(steps):
                qk(steps[idx + 1])
            ex(st)
            pv(st)
            yield

    def conv_gen(self, l, n, st, V):
        P = self.P
        hTk = [("hT", k) for k in range(16)]
        nh, hl = _halves(n + 2)
        N2 = n + 2
        for j in range(16):
            s, ci = st.next()
            bx = P.ps(nh, "C")
            mms = [(self.PS[:, bx + h, 0:hl], self.wsl[s][:, ci * 16 + k, :], self.hT[:, k, h * hl:(h + 1) * hl],
                    k == 0, k == 15) for k in range(16) for h in range(nh)]
            self.mm(mms, [("w", s)] + hTk, self.pskeys(bx, nh))
            t_cx = P.tmp()
            self.tt(self.v3(self.Tr[:, t_cx, 0:N2], nh), self.psv(bx, nh, hl), self.v3(self.valb[:, 0:N2], nh), ALU.mult,
                    self.pskeys(bx, nh) + [("val",)], [("T", t_cx)])
            yield
            s, ci = st.next()
            bc = P.ps(nh, "C")
            mms = [(self.PS[:, bc + h, 0:hl], self.wsl[s][:, ci * 16 + k, :], self.hT[:, k, h * hl:(h + 1) * hl],
                    k == 0, k == 15) for k in range(16) for h in range(nh)]
            self.mm(mms, [("w", s)] + hTk, self.pskeys(bc, nh))
            t_u = P.tmp()
            self.tt(self.v3(self.Tr[:, t_u, 0:N2], nh), self.psv(bc, nh, hl), self.v3(self.Tr[:, t_cx, 0:N2], nh), ALU.mult,
                    self.pskeys(bc, nh) + [("T", t_cx)], [("T", t_u)])
            u = self.Tr[:, t_u, :]
            cw = lambda tap, j=j: V[:, VC_CW + 3 * j + tap:VC_CW + 3 * j + tap + 1]
            t0 = P.tmp()
            P.op("dve", [("T", t_u), ("vecs", l)], [("T", t0)],
                 lambda e, o=self.Tr[:, t0, 0:n], i_=u[:, 1:1 + n], s1=cw(1), s2=V[:, VC_CB + j:VC_CB + j + 1]:
                 e.tensor_scalar(out=o, in0=i_, scalar1=s1, scalar2=s2, op0=ALU.mult, op1=ALU.add))
            ta = P.tmp()
            self.stt(self.Tr[:, ta, 0:n], u[:, 0:n], cw(0), self.Tr[:, t0, 0:n], ALU.mult, ALU.add,
                     [("T", t_u), ("T", t0), ("vecs", l)], [("T", ta)])
            tb = P.tmp()
            self.stt(self.Tr[:, tb, 0:n], u[:, 2:2 + n], cw(2), self.Tr[:, ta, 0:n], ALU.mult, ALU.add,
                     [("T", t_u), ("T", ta), ("vecs", l)], [("T", tb)])
            yield
            s, ci = st.next()
            bb = P.ps(1, "C")
            mms = [(self.PS[:, bb, 0:n], self.wsl[s][:, ci * 16 + k, :], self.hT[:, k, 1:1 + n], k == 0, k == 15)
                   for k in range(16)]
            self.mm(mms, [("w", s)] + hTk, [("ps", bb)])
            self.tt(self.yT[:, j, 0:n], self.PS[:, bb, 0:n], self.Tr[:, tb, 0:n], ALU.mult,
                    [("ps", bb), ("T", tb)], [("y", j)])
            yield

    @staticmethod
    def interleave(ga, na, gb, nb):
        done_b = 0
        for i in range(na):
            next(ga)
            want = ((i + 1) * nb) // na
            while done_b < want:
                next(gb)
                done_b += 1
        for _ in ga:
            pass
        for _ in gb:
            pass

    def tile_pass(self, l, t, src, a, n, dst, dst_a, ctx_mode, kv_only=False, final=False, skip_kv=False):
        P = self.P
        V = self.vecs[l]
        hTk = [("hT", k) for k in range(16)]
        if kv_only:
            self.load_x(src, a, n, 0)
            self.norm(l, t, 0, 1, n)
            self.kv(l, n, False, lambda m: self.Kc[:, m, 0:n], ("Kc",), lambda blk: self.Vc[:, blk, :], ("Vc",), 0)
            return
        self.stage("tile_load")
        self.load_x(src, a, n, 1)
        if not ctx_mode:
            self.load_tabs(a, n, 1)
        else:
            P.op("dve", [], [("val",)], lambda e: e.memset(self.valb[:, 0:n + 2], 1.0))
            P.op("dve", [("val",)], [("val",)], lambda e: e.memset(self.valb[:, 0:1], 0.0))
            P.op("dve", [("val",)], [("val",)], lambda e: e.memset(self.valb[:, n + 1:n + 2], 0.0))
        self.stage("norm1")
        self.norm(l, t, 0, 0, n + 2)
        tag = "L%d%s%d" % (l, "c" if ctx_mode else "m", a)
        K16 = lambda nm: [(nm, c) for c in range(16)]
        self.dump(tag, "hT", self.hT[:, :, 0:n + 2], [128, 16, n + 2], BF16, K16("hT"))
        self.dump(tag, "mod", self.mod[:], [128, 2, 96], F32, [("mod",)])
        self.stage("kvctx")
        if ctx_mode and not skip_kv:
            self.kv(l, n, False, lambda m: self.Kc[:, m, 0:n], ("Kc",), lambda blk: self.Vc[:, blk, :], ("Vc",), 0)
        st = Kern.Stream(self, l, LD_MAIN0)
        self.stage("q")
        pend = None
        for c in range(16):
            s, ci = st.next()
            b = P.ps(1)
            mms = [(self.PS[:, b, 0:n], self.wsl[s][:, ci * 16 + k, :], self.hT[:, k, 1:1 + n], k == 0, k == 15)
                   for k in range(16)]
            if c == 0:
                for k in range(16):
                    self.mm([mms[k]], [("w", s), ("hT", k)], [("ps", b)])
            else:
                self.mm(mms, [("w", s)] + hTk, [("ps", b)])
            if ctx_mode:
                self.act(self.qy[:, c, 0:n], self.PS[:, b, 0:n], AF.Copy, [("ps", b)], [("qy", c)])
            else:
                ra = self.rope_a(b, n)
                if pend is not None:
                    self.rope_b(pend[0], self.qy[:, pend[1], 0:n], [("qy", pend[1])])
                pend = (ra, c)
        if pend is not None:
            self.rope_b(pend[0], self.qy[:, pend[1], 0:n], [("qy", pend[1])])
        self.dump(tag, "q", self.qy[:, :, 0:n], [128, 16, n], BF16, K16("qy"))
        if ctx_mode:
            self.dump(tag, "Kc", self.Kc[:], [128, 2, CTX], BF16, [("Kc",)])
            self.dump(tag, "Vc", self.Vc[:], [128, 2, 256], BF16, [("Vc",)])
        self.stage("attn")
        nblk = n // 128
        na = nblk * 4 * (2 if ctx_mode else 5)
        sv_allow, self.allow_ada = self.allow_ada, False
        self.interleave(self.attention_gen(nblk, a, ctx_mode), na, self.conv_gen(l, n, st, V), 48)
        self.allow_ada = sv_allow
        self.dump(tag, "at", self.at[:, :, 0:n], [128, 16, n], BF16, K16("at"))
        self.dump(tag, "y", self.yT[:, :, 0:n], [128, 16, n], BF16, K16("y"))
        self.stage("merge")
        for j in range(16):
            bs = []
            for which in range(4):
                s, ci = st.next()
                b = P.ps(1)
                src_t, keyn = [(self.hT, "hT"), (self.hT, "hT"), (self.at, "at"), (self.yT, "y")][which]
                off = 1 if which < 2 else 0
                mms = [(self.PS[:, b, 0:n], self.wsl[s][:, ci * 16 + k, :], src_t[:, k, off:off + n], k == 0, k == 15)
                       for k in range(16)]
                self.mm(mms, [("w", s)] + [(keyn, k) for k in range(16)], [("ps", b)])
                bs.append(b)
            bga, bgc, bao, bco = bs
            tg = P.tmp()
            self.act(self.Tr[:, tg, 0:n], self.PS[:, bga, 0:n], AF.Sigmoid, [("ps", bga)], [("T", tg)])
            tc = P.tmp()
            self.act(self.Tr[:, tc, 0:n], self.PS[:, bgc, 0:n], AF.Sigmoid, [("ps", bgc)], [("T", tc)])
            t1 = P.tmp()
            self.tt(self.Tr[:, t1, 0:n], self.PS[:, bao, 0:n], self.Tr[:, tg, 0:n], ALU.mult, [("ps", bao), ("T", tg)], [("T", t1)])
            t2 = P.tmp()
            self.tt(self.Tr[:, t2, 0:n], self.PS[:, bco, 0:n], self.Tr[:, tc, 0:n], ALU.mult, [("ps", bco), ("T", tc)], [("T", t2)])
            self.tt(self.mT[:, j, 0:n], self.Tr[:, t1, 0:n], self.Tr[:, t2, 0:n], ALU.add, [("T", t1), ("T", t2)], [("m", j)])
        self.dump(tag, "m", self.mT[:, :, 0:n], [128, 16, n], BF16, K16("m"))
        self.stage("wo")
        self.ada_need(l, "wo")
        ns2 = self.norm_begin(n, "C")
        self.ring = "six"
        sv_allow, self.allow_ada = self.allow_ada, False
        for j in range(16):
            s, ci = st.next()
            b = P.ps(1, "six")
            mms = [(self.PS[:, b, 0:n], self.wsl[s][:, ci * 16 + k, :], self.mT[:, k, 0:n], k == 0, k == 15)
                   for k in range(16)]
            self.mm(mms, [("w", s)] + [("m", k) for k in range(16)], [("ps", b)])
            xs = self.xt[:, j, 1:1 + n]
            self.stt(xs, self.PS[:, b, 0:n], self.mod[:, t, 32 + j:33 + j], xs, ALU.mult, ALU.add,
                     [("ps", b), ("xt", j), ("mod",)], [("xt", j)])
            if j >= 1:
                self.norm_stat(ns2, j - 1, 1)
        self.norm_stat(ns2, 15, 1)
        self.allow_ada = sv_allow
        self.dump(tag, "xmid", self.xt[:, :, 1:1 + n], [128, 16, n], F32, K16("xt"))
        self.stage("ffn")
        self.ada_need(l, "norm2")
        self.norm_finish(ns2, l, t, 1, 1)
        self.ring = "all"
        for j in range(NFF):
            bs = []
            for which in range(2):
                s, ci = st.next()
                b = P.ps(1)
                mms = [(self.PS[:, b, 0:n], self.wsl[s][:, ci * 16 + k, :], self.hT[:, k, 1:1 + n], k == 0, k == 15)
                       for k in range(16)]
                if j == 0 and which == 0:
                    for k in range(16):
                        self.mm([mms[k]], [("w", s), ("hT", k)], [("ps", b)])
                else:
                    self.mm(mms, [("w", s)] + hTk, [("ps", b)])
                bs.append(b)
            tg = P.tmp()
            self.act(self.Tr[:, tg, 0:n], self.PS[:, bs[0], 0:n], AF.Silu, [("ps", bs[0])], [("T", tg)])
            self.tt(self.actT[:, j, 0:n], self.PS[:, bs[1], 0:n], self.Tr[:, tg, 0:n], ALU.mult,
                    [("ps", bs[1]), ("T", tg)], [("act", j)])
        self.ada_need(l, "ffo")
        for j in range(16):
            s = self.wload_main(l, LD_FO0 + j, NFF * 128)
            b = P.ps(1)
            mms = [(self.PS[:, b, 0:n], self.wsl[s][:, k, :], self.actT[:, k, 0:n], k == 0, k == NFF - 1)
                   for k in range(NFF)]
            self.mm(mms, [("w", s)] + [("act", k) for k in range(NFF)], [("ps", b)])
            xs = self.xt[:, j, 1:1 + n]
            self.stt(xs, self.PS[:, b, 0:n], self.mod[:, t, 80 + j:81 + j], xs, ALU.mult, ALU.add,
                     [("ps", b), ("xt", j), ("mod",)], [("xt", j)])
            if j % 4 == 3 and not final and tag not in self.debug:
                self.store_piece(dst, dst_a, n, j // 4, final)
        self.dump(tag, "xout", self.xt[:, :, 1:1 + n], [128, 16, n], F32, K16("xt"))
        self.stage("store")
        if not final and tag in self.debug:
            for p in range(4):
                self.store_piece(dst, dst_a, n, p, final)
        if final:
            self.norm(l, t, 0, 1, n, final=True)
        if final:
            for p in range(4):
                self.store_piece(dst, dst_a, n, p, final)

    def store_piece(self, dst, dst_a, n, p, final):
        dv = dst.rearrange("(c p) t -> p c t", p=128)
        self.dma("sp", dv[:, 4 * p:4 * p + 4, dst_a:dst_a + n], self.xt[:, 4 * p:4 * p + 4, 1:1 + n],
                 [("xt", c) for c in range(4 * p, 4 * p + 4)],
                 [("x1dram", p)] if not final else [("outdram", p)], self.s_st[p])

    def layer(self, l):
        last = l == 1
        self.allow_ada = False
        self.ada(l)
        csrc = self.ctxT if l == 0 else self.xcT
        sk = self.wload(self.ws[l][LD_K], 4096)
        sv = self.wload(self.ws[l][LD_V], 4096)
        self.kv_slots = (sk, sv)
        self.pinned = {sk, sv}
        self.tile_pass(l, 1, csrc, 1, CTX, self.xcT, 1, True, kv_only=True)
        self.stage("prepass")
        src = self.xT if l == 0 else self.x1T
        lo, hi = (0, NTOK) if l == 0 else (128, NTOK - 128)
        tiles = []
        a = lo
        while a < hi:
            n = min(512, hi - a)
            tiles.append((a, n))
            a += n
        self.prepass_all(l, src, tiles)
        self.kv_slots = None
        self.pinned = set()
        self.stage("main")
        if l == 0 and 1 in self.layers:
            self.ada_queue = self.ada_queue + [(1, i) for i in range(32)]
        lo, hi = (128, NTOK - 128) if l == 0 else (256, NTOK - 256)
        a = lo
        ti = 0
        while a < hi:
            n = min(512, hi - a)
            if last:
                self.tile_pass(l, 0, src, a, n, self.outT, a - 256, False, final=True)
            else:
                self.wb_tile = ti if ti < 4 else None
                self.use_bf16 = ti >= 4
                self.allow_ada = ti in (0, 2, 3)
                self.ada_period = 4 if ti == 0 else 3
                self.tile_pass(l, 0, src, a, n, self.x1T, a, False)
            a += n
            ti += 1
        if not last:
            self.wb_tile = None
            self.use_bf16 = True
            self.allow_ada = False
            self.tile_pass(l, 1, self.ctxT, 1, CTX, self.xcT, 1, True, skip_kv=True)
            self.use_bf16 = False
            self.allow_ada = False


_CACHE = {}


def _get_prog():
    if "k" not in _CACHE:
        _CACHE["k"] = Kern()
    return _CACHE["k"]


def host_inputs(x, c, ctx, c_ctx, ada_w, ada_b, norm1_g, norm2_g, w_in, conv_w, conv_b, sink,
                w_attn_out, w_conv_out, w_o, w_ffn_in, w_ffn_out, final_g, cores=range(8)):
    f = lambda a: np.ascontiguousarray(np.asarray(a, dtype=np.float32))
    x, c, ctx, c_ctx, ada_w, ada_b, norm1_g, norm2_g, w_in, conv_w, conv_b, sink, w_attn_out, w_conv_out, w_o, \
        w_ffn_in, w_ffn_out, final_g = map(f, (x, c, ctx, c_ctx, ada_w, ada_b, norm1_g, norm2_g, w_in, conv_w, conv_b,
                                               sink, w_attn_out, w_conv_out, w_o, w_ffn_in, w_ffn_out, final_g))
    shared = {}
    for l in range(2):
        shared["ws%d" % l] = _build_wstream(l, w_in, w_attn_out, w_conv_out, w_o, w_ffn_in, w_ffn_out)
        shared["wa%d" % l] = _build_wada(l, ada_w)
        shared["vec%d" % l] = _build_vec(l, ada_b, norm1_g, norm2_g, conv_w, conv_b, sink, final_g)
    in_maps = []
    for r in cores:
        b, s = divmod(r, 4)
        base = 2048 * s - 256
        xT = np.zeros((D, NTOK), np.float32)
        lo, hi = max(base, 0), min(base + NTOK, SEQ)
        xT[:, lo - base:hi - base] = x[b, lo:hi, :].T
        ctxT = np.zeros((D, CTX + 2), np.float32)
        ctxT[:, 1:CTX + 1] = ctx[b].T
        cvec = np.zeros((128, 32), np.float32)
        cvec[:, 0::2] = _fm(c[b])
        cvec[:, 1::2] = _fm(c_ctx)
        m = dict(shared)
        m.update({"xT": xT, "ctxT": ctxT, "cvec": cvec, "tab": _build_tab(base)})
        in_maps.append(m)
    return in_maps


def kernel(**inputs):
    in_maps = host_inputs(**inputs)
    k = _get_prog()
    res = run_bass_kernel_spmd(k.nc, in_maps, core_ids=list(range(8)))
    out = np.zeros((2, SEQ, D), np.float32)
    for r in range(8):
        b, s = divmod(r, 4)
        out[b, 2048 * s:2048 * s + 2048, :] = res.results[r]["outT"].T
    return out
```

```python
import contextlib
import numpy as np
import concourse.bass as bass
import concourse.mybir as mybir
from concourse.bass_utils import run_bass_kernel_spmd

F32 = mybir.dt.float32
BF16 = mybir.dt.bfloat16
AF = mybir.ActivationFunctionType
ALU = mybir.AluOpType

D = 2048
NCH = 16
DFF = 5632
NFF = 44
SEQ = 8192
CTX = 256
NTOK = 2560
GRID_W = 64
SLOT = 6144
NSLOT = 4
NT = 7
NB = 6
EPS = 1e-6
NEGB = -30000.0
N_MAIN_CHUNKS = 16 + 48 + 64 + 16 + 88
N_MAIN_LOADS = (N_MAIN_CHUNKS + 2) // 3
LD_K, LD_V, LD_MAIN0 = 0, 1, 2
LD_FO0 = LD_MAIN0 + N_MAIN_LOADS
N_LOADS = LD_FO0 + 16
VC_ADAB, VC_N1G, VC_N2G, VC_CW, VC_CB, VC_SINK, VC_FG, NVEC = 0, 96, 112, 128, 176, 192, 208, 224
TB_COS, TB_SIN, TB_VAL = 0, NTOK, 2 * NTOK
TB_KB = 3 * NTOK
TB_TRI = TB_KB + 20
TB_PERM = TB_TRI + 1024
TB_ID = TB_PERM + 128
NTAB = TB_ID + 128


def _qperm():
    idx = np.zeros(D, np.int64)
    for c in range(16):
        m, i = divmod(c, 8)
        for p in range(128):
            head = 16 * m + i + 8 * (p // 64)
            idx[c * 128 + p] = head * 64 + (p % 64)
    return idx


def _tile_cols(W):
    K, n = W.shape
    return W.reshape(K // 128, 128, n).transpose(1, 0, 2)


def _build_wstream(l, w_in, w_attn_out, w_conv_out, w_o, w_ffn_in, w_ffn_out):
    qp = _qperm()
    wi = w_in[l]
    ws = np.zeros((N_LOADS, 128, SLOT), np.float32)
    q = wi[:, 0:2048][:, qp]
    k = wi[:, 2048:2304]
    v = wi[:, 2304:2560]
    cb, cc, cx = wi[:, 2560:4608], wi[:, 4608:6656], wi[:, 6656:8704]
    ga, gc = wi[:, 8704:10752], wi[:, 10752:12800]
    ao = w_attn_out[l][qp, :]
    co = w_conv_out[l]
    wo = w_o[l]
    fi = w_ffn_in[l]
    fo = w_ffn_out[l]

    def chunk(W, j):
        return _tile_cols(W[:, 128 * j:128 * j + 128])

    ws[LD_K, :, 0:4096] = np.stack([chunk(k, 0), chunk(k, 1)], 1).reshape(128, -1)
    ws[LD_V, :, 0:4096] = _tile_cols(v).reshape(128, -1)
    chunks = []
    for j in range(16):
        chunks.append(chunk(q, j))
    for j in range(16):
        chunks += [chunk(cx, j), chunk(cc, j), chunk(cb, j)]
    for j in range(16):
        chunks += [chunk(ga, j), chunk(gc, j), chunk(ao, j), chunk(co, j)]
    for j in range(16):
        chunks.append(chunk(wo, j))
    for j in range(NFF):
        chunks += [chunk(fi[:, 0:DFF], j), chunk(fi[:, DFF:], j)]
    assert len(chunks) == N_MAIN_CHUNKS
    for i in range(N_MAIN_LOADS):
        grp = chunks[3 * i:3 * i + 3]
        ws[LD_MAIN0 + i, :, 0:len(grp) * 2048] = np.stack(grp, 1).reshape(128, -1)
    for j in range(16):
        ws[LD_FO0 + j, :, 0:NFF * 128] = _tile_cols(fo[:, 128 * j:128 * j + 128]).reshape(128, -1)
    return ws


def _build_wada(l, ada_w):
    wa = np.zeros((32, 128, SLOT), np.float32)
    for i in range(32):
        grp = [_tile_cols(ada_w[l][:, 128 * (3 * i + t):128 * (3 * i + t) + 128]) for t in range(3)]
        wa[i] = np.stack(grp, 1).reshape(128, -1)
    return wa


def _fm(vec):
    return vec.reshape(-1, 128).T


def _build_vec(l, ada_b, norm1_g, norm2_g, conv_w, conv_b, sink, final_g):
    v = np.zeros((128, NVEC), np.float32)
    v[:, VC_ADAB:VC_ADAB + 96] = _fm(ada_b[l])
    v[:, VC_N1G:VC_N1G + 16] = _fm(norm1_g[l])
    v[:, VC_N2G:VC_N2G + 16] = _fm(norm2_g[l])
    cw = np.stack([_fm(conv_w[l][t]) for t in range(3)], -1)
    v[:, VC_CW:VC_CW + 48] = cw.reshape(128, 48)
    v[:, VC_CB:VC_CB + 16] = _fm(conv_b[l])
    qp = _qperm()
    v[:, VC_SINK:VC_SINK + 16] = _fm(sink[l][qp // 64])
    v[:, VC_FG:VC_FG + 16] = _fm(final_g)
    return v


def _build_tab(base):
    tab = np.zeros((128, NTAB), np.float32)
    t = base + np.arange(NTOK)
    row = (t // GRID_W).astype(np.float32)
    col = (t % GRID_W).astype(np.float32)
    freqs = (np.float32(10000.0) ** (-np.arange(16, dtype=np.float32) / np.float32(16))).astype(np.float32)
    for p in range(128):
        d = p % 64
        e = d % 32
        pos = row if d < 32 else col
        ang = (pos * freqs[e % 16]).astype(np.float32)
        tab[p, TB_COS:TB_COS + NTOK] = np.cos(ang)
        s = np.sin(ang)
        tab[p, TB_SIN:TB_SIN + NTOK] = -s if e < 16 else s
        pe = e + 16 if e < 16 else e - 16
        tab[p - e + pe, TB_PERM + p] = 1.0
        tab[p, TB_ID + p] = 1.0
    valid = ((t >= 0) & (t < SEQ)).astype(np.float32)
    tab[:, TB_VAL:TB_VAL + NTOK] = valid[None, :]
    tab[:, TB_KB:TB_KB + 20] = np.where(valid.reshape(20, 128).T > 0, 0.0, NEGB)
    jj = np.arange(128)[:, None]
    ii = np.arange(128)[None, :]
    tab[:, TB_TRI:TB_TRI + 512] = np.tile((jj >= ii).astype(np.float32), (1, 4))
    tab[:, TB_TRI + 512:TB_TRI + 1024] = np.tile((jj <= ii).astype(np.float32), (1, 4))
    return tab


class Sem:
    def __init__(self, h, name):
        self.h = h
        self.name = name
        self.count = 0


class Eng:
    def __init__(self, name, sem):
        self.name = name
        self.sem = sem
        self.ops = []
        self.waited = {}


class Prog:
    def __init__(self, nc, es):
        self.nc = nc
        self.es = es
        self.engs = {}
        for n in ["pe", "act", "dve", "pool", "sp"]:
            self.engs[n] = Eng(n, self.new_sem("c_" + n))
        self.lastw = {}
        self.readers = {}
        self.ps_ptr = {"all": 0, "A": 0, "B": 0, "C": 0, "six": 0}
        self.t_ptr = 0
        self.b_ptr = 0

    def new_sem(self, name):
        return Sem(self.es.enter_context(self.nc.semaphore(name)), name)

    def sbuf(self, name, shape, dt):
        return self.es.enter_context(self.nc.sbuf_tensor(name, shape, dt))

    def op(self, eng, reads, writes, fn, dsem=None):
        E = self.engs[eng]
        deps = {}
        psr = [k for k in reads if k[0] == "ps"]
        if psr:
            reads = [k for k in reads if k[0] != "ps"]
            writes = list(writes) + [k for k in psr if k not in writes]

        def add(sv):
            if sv is not None:
                s, v = sv
                if deps.get(s, 0) < v:
                    deps[s] = v
        for k in reads:
            add(self.lastw.get(k))
        for k in writes:
            add(self.lastw.get(k))
            for s, v in self.readers.get(k, {}).items():
                add((s, v))
        waits = []
        for s, v in deps.items():
            if s is E.sem and eng == "pe":
                continue
            if E.waited.get(s, 0) >= v:
                continue
            E.waited[s] = v
            waits.append((s, v))
        if dsem is None:
            E.sem.count += 1
            done = (E.sem, E.sem.count)
            inc = (E.sem, 1)
        else:
            dsem.count += 16
            done = (dsem, dsem.count)
            inc = (dsem, 16)
        E.ops.append((waits, fn, inc))
        for k in reads:
            r = self.readers.setdefault(k, {})
            if r.get(done[0], 0) < done[1]:
                r[done[0]] = done[1]
        for k in writes:
            self.lastw[k] = done
            self.readers[k] = {}
        return done

    def wait_all(self, eng, sems):
        E = self.engs[eng]
        waits = [(s, s.count) for s in sems if s.count > 0]
        E.ops.append((waits, None, None))

    def ps(self, n=1, pool="all"):
        lo, size = {"all": (0, 8), "A": (0, 4), "B": (4, 2), "C": (6, 2), "six": (0, 6)}[pool]
        p = self.ps_ptr[pool]
        if n == 2 and p % 2 == 1:
            p += 1
        if p + n > size:
            p = 0
        self.ps_ptr[pool] = (p + n) % size
        return lo + p

    def tmp(self):
        i = self.t_ptr
        self.t_ptr = (i + 1) % NT
        return i

    def btmp(self):
        i = self.b_ptr
        self.b_ptr = (i + 1) % NB
        return i

    def bpair(self):
        i = self.b_ptr
        if i % 2 == 1:
            i += 1
        if i + 2 > NB:
            i = 0
        self.b_ptr = (i + 2) % NB
        return i

    def emit(self, block):
        def runner(E):
            def run(e):
                for waits, fn, inc in E.ops:
                    for s, v in waits:
                        e.wait_ge(s.h, v)
                    if fn is not None:
                        ins = fn(e)
                        ins.then_inc(inc[0].h, inc[1])
            return run
        block.tensor(runner(self.engs["pe"]))
        block.scalar(runner(self.engs["act"]))
        block.vector(runner(self.engs["dve"]))
        block.gpsimd(runner(self.engs["pool"]))
        block.sync(runner(self.engs["sp"]))


def _halves(n):
    if n <= 512:
        return 1, n
    assert n % 2 == 0
    return 2, n // 2


class StopBuild(Exception):
    pass


class Kern:
    def __init__(self, layers=(0, 1), debug=None, stop=None):
        self.layers = layers
        self.debug = debug or {}
        self.stop = stop
        self.nstage = 0
        nc = bass.Bass("TRN2", target_bir_lowering=False)
        self.nc = nc
        self.es = contextlib.ExitStack()
        with self.es:
            self.build()

    def build(self):
        nc = self.nc
        P = Prog(nc, self.es)
        self.P = P
        dt = nc.dram_tensor
        self.xT = dt("xT", [D, NTOK], F32, kind="ExternalInput").ap()
        self.ctxT = dt("ctxT", [D, CTX + 2], F32, kind="ExternalInput").ap()
        self.cvec = dt("cvec", [128, 32], F32, kind="ExternalInput").ap()
        self.tab = dt("tab", [128, NTAB], F32, kind="ExternalInput").ap()
        self.vec = [dt("vec%d" % l, [128, NVEC], F32, kind="ExternalInput").ap() for l in range(2)]
        self.ws = [dt("ws%d" % l, [N_LOADS, 128, SLOT], F32, kind="ExternalInput").ap() for l in range(2)]
        self.wa = [dt("wa%d" % l, [32, 128, SLOT], F32, kind="ExternalInput").ap() for l in range(2)]
        self.outT = dt("outT", [D, 2048], F32, kind="ExternalOutput").ap()
        self.x1T = dt("x1T", [D, NTOK], F32, kind="Internal").ap()
        self.xcT = dt("xcT", [D, CTX + 2], F32, kind="Internal").ap()
        self.wsb = dt("wsb", [N_LOADS, 128, SLOT], BF16, kind="Internal").ap()
        self.wb_tile = None
        self.use_bf16 = False
        self.dbg_out = {}
        self.dbg_sems = []

        sb = P.sbuf
        self.xt = sb("xt", [128, 16, 514], F32)
        self.hT = sb("hT", [128, 16, 514], BF16)
        self.big = sb("big", [128, 24576], BF16)
        self.wsl = [sb("wsl%d" % i, [128, 48, 128], BF16) for i in range(NSLOT)]
        self.Ksb = sb("Ksb", [128, 2, NTOK], BF16)
        self.Vsb = sb("Vsb", [128, 20, 256], BF16)
        self.Kc = sb("Kc", [128, 2, CTX], BF16)
        self.Vc = sb("Vc", [128, 2, 256], BF16)
        self.cosb = sb("cosb", [128, 512], F32)
        self.sinb = sb("sinb", [128, 512], F32)
        self.valb = sb("valb", [128, 514], F32)
        self.Tr = sb("Tr", [128, NT, 514], F32)
        self.rstd = sb("rstd", [128, 514], F32)
        self.Br = sb("Br", [128, NB, 514], BF16)
        self.sexp = sb("sexp", [128, 16], F32)
        self.trif = sb("trif", [128, 2, 512], F32)
        self.maskb = sb("maskb", [128, 2, 512], BF16)
        self.identf = sb("identf", [128, 128], F32)
        self.ident = sb("ident", [128, 128], BF16)
        self.kbias = sb("kbias", [128, 20], F32)
        self.permf = sb("permf", [128, 128], F32)
        self.perm = sb("perm", [128, 128], BF16)
        self.ones64 = sb("ones64", [128, 64], BF16)
        self.onesK = sb("onesK", [128, 128], BF16)
        self.epst = sb("epst", [128, 1], F32)
        self.vecs = [sb("vecs%d" % l, [128, NVEC], F32) for l in range(2)]
        self.cv = sb("cv", [128, 32], F32)
        self.cs = sb("cs", [128, 32], BF16)
        self.mod = sb("mod", [128, 2, 96], F32)
        self.modraw = sb("modraw", [128, 2, 96, 2], F32)
        self.ada_queue = [(0, i) for i in range(32)]
        self.ada_todo = {}
        self.allow_ada = False
        self.ada_period = 3
        self.n_wloads = 0
        self.gs = sb("gs", [128, 2, 2, 16], F32)
        self.PS = self.es.enter_context(nc.psum_tensor("PS", [128, 8, 512], F32))
        self.s_w = [P.new_sem("w%d" % i) for i in range(NSLOT)]
        self.s_wb = [P.new_sem("wb%d" % i) for i in range(NSLOT)]
        self.s_ldx = [P.new_sem("ldx%d" % i) for i in range(4)]
        self.s_ldx2 = [P.new_sem("ldy%d" % i) for i in range(4)]
        self.s_cos = P.new_sem("ldcos")
        self.s_sin = P.new_sem("ldsin")
        self.s_val = P.new_sem("ldval")
        self.s_st = [P.new_sem("st%d" % i) for i in range(4)]
        self.s_c = P.new_sem("cst")
        self.w_ptr = 0
        self.ring = "all"
        self.pinned = set()
        self.kv_slots = None
        self.qy = self.big[:, 0:8192].rearrange("p (c t) -> p c t", t=512)
        self.at = self.big[:, 8192:16384].rearrange("p (c t) -> p c t", t=512)
        self.yT = self.big[:, 16384:24576].rearrange("p (c t) -> p c t", t=512)
        self.mT = self.qy
        self.actT = self.big[:, 0:NFF * 512].rearrange("p (c t) -> p c t", t=512)
        self.xt2 = self.big[:, 0:2 * 16 * 514].bitcast(F32).rearrange("p (c t) -> p c t", t=514)

        try:
            self.consts()
            for l in self.layers:
                self.layer(l)
        except StopBuild:
            pass
        P.wait_all("sp", self.s_st + self.dbg_sems)
        with nc.Block() as block:
            P.emit(block)

    def dump(self, tag, name, ap, shape, dtype, reads):
        if tag not in self.debug:
            return
        full = "dbg_%s_%s" % (tag, name)
        d = self.nc.dram_tensor(full, list(shape), dtype, kind="ExternalOutput").ap()
        self.dbg_out[full] = d
        sem = self.P.new_sem("s_" + full)
        self.dma("sp", d, ap, reads, [], sem)
        self.dbg_sems.append(sem)

    def stage(self, name):
        self.nstage += 1
        if self.stop is not None and name == self.stop:
            raise StopBuild()

    def dma(self, q, out, in_, reads, writes, dsem):
        return self.P.op(q, reads, writes, lambda e, o=out, i=in_: e.dma_start(out=o, in_=i), dsem=dsem)

    def next_slot(self):
        s = self.w_ptr
        while s in self.pinned:
            s = (s + 1) % NSLOT
        self.w_ptr = (s + 1) % NSLOT
        return s

    def wload(self, src_ap, n):
        s = self.next_slot()
        flat = self.wsl[s][:].rearrange("p a b -> p (a b)")
        self.dma("pool", flat[:, 0:n], src_ap[:, 0:n], [], [("w", s)], self.s_w[s])
        return s

    def wload_main(self, l, lid, n):
        if l == 0 and (self.use_bf16 or (self.wb_tile is not None and lid % 4 < self.wb_tile)):
            s = self.next_slot()
            flat = self.wsl[s][:].rearrange("p a b -> p (a b)")
            self.dma("pool", flat[:, 0:n], self.wsb[lid][:, 0:n], [("wsb", lid)], [("w", s)], self.s_w[s])
            return s
        s = self.wload(self.ws[l][lid], n)
        if l == 0 and self.wb_tile is not None and lid % 4 == self.wb_tile:
            flat = self.wsl[s][:].rearrange("p a b -> p (a b)")
            self.dma("sp", self.wsb[lid][:, 0:n], flat[:, 0:n], [("w", s)], [("wsb", lid)], self.s_wb[s])
        return s

    def mm(self, mms, reads, writes):
        def fn(e, mms=mms):
            ins = None
            for (o, l, r, st, sp_) in mms:
                ins = e.matmul(o, lhsT=l, rhs=r, start=st, stop=sp_)
            return ins
        return self.P.op("pe", reads, writes, fn)

    def act(self, out, in_, func, reads, writes, bias=None, scale=None):
        kw = {}
        if bias is not None:
            kw["bias"] = bias
        if scale is not None:
            kw["scale"] = scale
        return self.P.op("act", reads, writes,
                         lambda e, o=out, i=in_, f=func, kw=kw: e.activation(out=o, in_=i, func=f, **kw))

    def tt(self, out, in0, in1, op, reads, writes, eng="dve"):
        return self.P.op(eng, reads, writes,
                         lambda e, o=out, a=in0, b=in1, op=op: e.tensor_tensor(out=o, in0=a, in1=b, op=op))

    def stt(self, out, in0, scalar, in1, op0, op1, reads, writes):
        return self.P.op("dve", reads, writes,
                         lambda e, o=out, a=in0, s=scalar, b=in1, o0=op0, o1=op1:
                         e.scalar_tensor_tensor(out=o, in0=a, scalar=s, in1=b, op0=o0, op1=o1))

    def psv(self, b, nh, hl, p0=0, p1=128):
        if nh == 1:
            return self.PS[p0:p1, b, 0:hl]
        return self.PS[p0:p1, b:b + nh, 0:hl]

    def v3(self, ap2d, nh):
        if nh == 1:
            return ap2d
        return ap2d.rearrange("p (h l) -> p h l", h=nh)

    def pskeys(self, b, nh):
        return [("ps", b + i) for i in range(nh)]

    def consts(self):
        P = self.P
        sp = "sp"
        self.dma(sp, self.trif[:].rearrange("p a b -> p (a b)"), self.tab[:, TB_TRI:TB_TRI + 1024], [], [("trif",)], self.s_c)
        self.dma(sp, self.identf[:], self.tab[:, TB_ID:TB_ID + 128], [], [("identf",)], self.s_c)
        self.dma(sp, self.kbias[:], self.tab[:, TB_KB:TB_KB + 20], [], [("kbias",)], self.s_c)
        self.dma(sp, self.permf[:], self.tab[:, TB_PERM:TB_PERM + 128], [], [("permf",)], self.s_c)
        self.dma(sp, self.cv[:], self.cvec[:, :], [], [("cv",)], self.s_c)
        for l in range(2):
            self.dma(sp, self.vecs[l][:], self.vec[l][:, :], [], [("vecs", l)], self.s_c)
        for key in [("trif",), ("identf",), ("kbias",), ("permf",), ("cv",), ("vecs", 0), ("vecs", 1)]:
            P.lastw[key] = (self.s_c, self.s_c.count)
        P.op("dve", [("permf",)], [("perm",)], lambda e: e.tensor_copy(out=self.perm[:], in_=self.permf[:]))
        P.op("dve", [("identf",)], [("ident",)], lambda e: e.tensor_copy(out=self.ident[:], in_=self.identf[:]))
        P.op("dve", [("trif",)], [("maskb",)],
             lambda e: e.tensor_scalar(out=self.maskb[:], in0=self.trif[:], scalar1=1.0e6, scalar2=-1.0e6,
                                       op0=ALU.mult, op1=ALU.add))
        P.op("dve", [], [("ones64",)], lambda e: e.memset(self.ones64[:], 1.0))
        P.op("dve", [], [("onesK",)], lambda e: e.memset(self.onesK[:], 1.0 / D))
        P.op("dve", [], [("eps",)], lambda e: e.memset(self.epst[:], EPS))
        self.act(self.cs[:], self.cv[:], AF.Silu, [("cv",)], [("cs",)])

    class Stream:
        def __init__(self, K, l, first_load):
            self.K, self.l, self.load, self.ci, self.slot = K, l, first_load, 3, None

        def next(self):
            if self.ci == 3:
                K = self.K
                K.n_wloads += 1
                if K.allow_ada and K.ada_queue and K.n_wloads % K.ada_period == 0:
                    K.ada_step(*K.ada_queue.pop(0))
                self.slot = K.wload_main(self.l, self.load, SLOT)
                self.load += 1
                self.ci = 0
            ci = self.ci
            self.ci += 1
            return self.slot, ci

    def ada_step(self, l, i):
        P = self.P
        b = P.ps(1, self.ring)
        psa = self.PS[:, b, 0:6].rearrange("p (c t) -> p c t", t=2)
        s = self.wload(self.wa[l][i], SLOT)
        for t in range(3):
            mms = [(psa[:, t, :], self.wsl[s][:, t * 16 + k, :], self.cs[:, 2 * k:2 * k + 2], k == 0, k == 15)
                   for k in range(16)]
            self.mm(mms, [("w", s), ("cs",)], [("ps", b)])
        P.op("dve", [("ps", b)], [("modraw", l)],
             lambda e, o=self.modraw[:, l, 3 * i:3 * i + 3, :], i_=psa: e.tensor_copy(out=o, in_=i_))

    def ada(self, l):
        self.stage("ada")
        while self.ada_queue and (self.ada_queue[0][0] != l or self.ada_queue[0][1] <= 10):
            self.ada_step(*self.ada_queue.pop(0))
        self.ada_fin(l, 0, 32)
        V = self.vecs[l]
        self.act(self.sexp[:], V[:, VC_SINK:VC_SINK + 16], AF.Exp, [("vecs", l)], [("se",)])
        self.ada_todo = {"wo": (15, 32, 48), "norm2": (26, 48, 80), "ffo": (31, 80, 96)}

    def ada_need(self, l, key):
        if key not in self.ada_todo:
            return
        step, lo, hi = self.ada_todo.pop(key)
        while self.ada_queue and self.ada_queue[0][0] == l and self.ada_queue[0][1] <= step:
            self.ada_step(*self.ada_queue.pop(0))
        self.ada_fin(l, lo, hi)

    def ada_fin(self, l, lo, hi):
        V = self.vecs[l]
        for t in range(2):
            self.tt(self.mod[:, t, lo:hi], self.modraw[:, l, lo:hi, t], V[:, VC_ADAB + lo:VC_ADAB + hi], ALU.add,
                    [("modraw", l), ("vecs", l)], [("mod",)])
        for t in range(2):
            if lo <= 16 and hi >= 32:
                self.stt(self.gs[:, t, 0, :], self.mod[:, t, 16:32], 1.0, V[:, VC_N1G:VC_N1G + 16], ALU.add, ALU.mult,
                         [("mod",), ("vecs", l)], [("gs",)])
            if lo <= 64 and hi >= 80:
                self.stt(self.gs[:, t, 1, :], self.mod[:, t, 64:80], 1.0, V[:, VC_N2G:VC_N2G + 16], ALU.add, ALU.mult,
                         [("mod",), ("vecs", l)], [("gs",)])

    def norm_begin(self, n, pool="all"):
        nh, hl = _halves(n)
        b = self.P.ps(nh, pool)
        return {"n": n, "nh": nh, "hl": hl, "b": b}

    def norm_stat(self, ns, c, c0, xb=None):
        P = self.P
        xbuf, xkey = xb or (self.xt, "xt")
        n, nh, hl, b = ns["n"], ns["nh"], ns["hl"], ns["b"]
        bi = P.btmp()
        sq = self.Br[:, bi, 0:n]
        self.act(sq, xbuf[:, c, c0:c0 + n], AF.Square, [(xkey, c)], [("B", bi)])
        mms = [(self.PS[:, b + h, 0:hl], self.onesK[:], self.Br[:, bi, h * hl:(h + 1) * hl], c == 0, c == 15)
               for h in range(nh)]
        self.mm(mms, [("B", bi), ("onesK",)], self.pskeys(b, nh))

    def norm_finish(self, ns, l, t, which, c0, final=False, xb=None):
        P = self.P
        xbuf, xkey = xb or (self.xt, "xt")
        n, nh, hl, b = ns["n"], ns["nh"], ns["hl"], ns["b"]
        pv = self.psv(b, nh, hl)
        t1 = P.tmp()
        self.act(self.v3(self.Tr[:, t1, 0:n], nh), pv, AF.Ln, self.pskeys(b, nh) + [("eps",)], [("T", t1)],
                 bias=self.epst[:, 0:1])
        rstd = self.rstd[:, 0:n]
        self.act(rstd, self.Tr[:, t1, 0:n], AF.Exp, [("T", t1)], [("rstd",)], scale=-0.5)
        V = self.vecs[l]
        for c in range(16):
            xs = xbuf[:, c, c0:c0 + n]
            if final:
                self.stt(xs, xs, V[:, VC_FG + c:VC_FG + c + 1], rstd, ALU.mult, ALU.mult,
                         [(xkey, c), ("rstd",), ("vecs", l)], [(xkey, c)])
                continue
            ti = P.tmp()
            self.stt(self.Tr[:, ti, 0:n], xs, self.gs[:, t, which, c:c + 1], rstd, ALU.mult, ALU.mult,
                     [(xkey, c), ("rstd",), ("gs",)], [("T", ti)])
            sh = self.mod[:, t, (0 if which == 0 else 48) + c:(0 if which == 0 else 48) + c + 1]
            self.act(self.hT[:, c, c0:c0 + n], self.Tr[:, ti, 0:n], AF.Identity, [("T", ti), ("mod",)], [("hT", c)],
                     bias=sh)

    def norm(self, l, t, which, c0, n, final=False, xb=None):
        ns = self.norm_begin(n)
        for c in range(16):
            self.norm_stat(ns, c, c0, xb)
        self.norm_finish(ns, l, t, which, c0, final, xb)

    def kv(self, l, n, rope, Kdst, Kkey, Vdst, Vkey, vblk0):
        P = self.P
        if self.kv_slots is not None:
            sk, sv = self.kv_slots
        else:
            sk = self.wload(self.ws[l][LD_K], 4096)
            sv = self.wload(self.ws[l][LD_V], 4096)
        wv = self.wsl[sv][:].rearrange("p a b -> p (a b)")[:, 0:4096].rearrange("p (k n) -> p k n", n=256)
        for m in range(2):
            b = P.ps(1, self.ring)
            mms = [(self.PS[:, b, 0:n], self.wsl[sk][:, m * 16 + k, :], self.hT[:, k, 1:1 + n], k == 0, k == 15)
                   for k in range(16)]
            self.mm(mms, [("w", sk)] + [("hT", k) for k in range(16)], [("ps", b)])
            if rope:
                self.stage("kv_rope")
                self.rope(b, n, Kdst(m), [Kkey])
                self.stage("kv_rope_done")
            else:
                self.act(Kdst(m), self.PS[:, b, 0:n], AF.Copy, [("ps", b)], [Kkey])
        for i in range(n // 128):
            b = P.ps(1, self.ring)
            mms = [(self.PS[:, b, 0:256], self.hT[:, k, 1 + 128 * i:1 + 128 * (i + 1)], wv[:, k, :], k == 0, k == 15)
                   for k in range(16)]
            self.mm(mms, [("w", sv)] + [("hT", k) for k in range(16)], [("ps", b)])
            self.act(Vdst(vblk0 + i), self.PS[:, b, 0:256], AF.Copy, [("ps", b)], [Vkey])

    def rope_a(self, b, n):
        P = self.P
        bi = P.btmp()
        qa = self.Br[:, bi, 0:n]
        self.act(qa, self.PS[:, b, 0:n], AF.Copy, [("ps", b)], [("B", bi)])
        t1 = P.tmp()
        self.tt(self.Tr[:, t1, 0:n], self.PS[:, b, 0:n], self.cosb[:, 0:n], ALU.mult, [("ps", b), ("cos_tab",)], [("T", t1)])
        return (bi, t1, n)

    def rope_b(self, state, dst, dkeys):
        P = self.P
        bi, t1, n = state
        qa = self.Br[:, bi, 0:n]
        b2 = P.ps(1, self.ring)
        self.mm([(self.PS[:, b2, 0:n], self.perm[:], qa, True, True)], [("B", bi), ("perm",)], [("ps", b2)])
        t2 = P.tmp()
        self.tt(self.Tr[:, t2, 0:n], self.PS[:, b2, 0:n], self.sinb[:, 0:n], ALU.mult, [("ps", b2), ("sin_tab",)], [("T", t2)])
        self.tt(dst, self.Tr[:, t1, 0:n], self.Tr[:, t2, 0:n], ALU.add, [("T", t1), ("T", t2)], dkeys)

    def rope(self, b, n, dst, dkeys):
        self.rope_b(self.rope_a(b, n), dst, dkeys)

    def load_tabs(self, a, n, halo):
        self.dma("sp", self.cosb[:, 0:n], self.tab[:, TB_COS + a:TB_COS + a + n], [], [("cos_tab",)], self.s_cos)
        self.dma("sp", self.sinb[:, 0:n], self.tab[:, TB_SIN + a:TB_SIN + a + n], [], [("sin_tab",)], self.s_sin)
        if halo:
            self.dma("sp", self.valb[:, 0:n + 2], self.tab[:, TB_VAL + a - 1:TB_VAL + a + n + 1], [], [("val",)], self.s_val)

    def load_x(self, src, a, n, halo, xb=None):
        xbuf, xkey = xb or (self.xt, "xt")
        sv = src.rearrange("(c p) t -> p c t", p=128)
        extra = []
        if xkey != "xt":
            for nm, cnt in (("qy", 16), ("at", 16), ("y", 16), ("m", 16), ("act", NFF)):
                extra += [(nm, c) for c in range(cnt)]
        for p in range(4):
            self.dma("sp", xbuf[:, 4 * p:4 * p + 4, 1 - halo:1 + n + halo], sv[:, 4 * p:4 * p + 4, a - halo:a + n + halo],
                     [("x1dram", q) for q in range(4)] if src is self.x1T or src is self.xcT else [],
                     [(xkey, c) for c in range(4 * p, 4 * p + 4)] + extra,
                     self.s_ldx[p] if xkey == "xt" else self.s_ldx2[p])

    def prepass_all(self, l, tiles):
        xb = lambda i: (self.xt2, "xt2") if i % 2 == 1 else None
        self.ring = "six"
        src0, a0, n0, _, _ = tiles[0]
        self.load_x(src0, a0, n0, 0, xb(0))
        ns = self.norm_begin(n0, "C")
        for c in range(16):
            self.norm_stat(ns, c, 1, xb(0))
        for i, (src, a, n, t, is_ctx) in enumerate(tiles):
            if not is_ctx:
                self.load_tabs(a, n, 0)
            nxt = tiles[i + 1] if i + 1 < len(tiles) else None
            if nxt:
                self.load_x(nxt[0], nxt[1], nxt[2], 0, xb(i + 1))
            self.norm_finish(ns, l, t, 0, 1, xb=xb(i))
            if nxt:
                ns = self.norm_begin(nxt[2], "C")
                for c in range(16):
                    self.norm_stat(ns, c, 1, xb(i + 1))
            for _ in range(2):
                if self.ada_queue and self.ada_queue[0][0] == l:
                    self.ada_step(*self.ada_queue.pop(0))
            if is_ctx:
                self.kv(l, n, False, lambda m, n=n: self.Kc[:, m, 0:n], ("Kc",), lambda blk: self.Vc[:, blk, :], ("Vc",), 0)
            else:
                self.kv(l, n, True, lambda m, a=a, n=n: self.Ksb[:, m, a:a + n], ("K",),
                        lambda blk: self.Vsb[:, blk, :], ("V",), a // 128)
        self.ring = "all"

    def prepass(self, l, src, a, n, idx=0):
        xb = (self.xt2, "xt2") if idx % 2 == 1 else None
        self.load_x(src, a, n, 0, xb)
        self.load_tabs(a, n, 0)
        self.stage("pp_norm")
        self.norm(l, 0, 0, 1, n, xb=xb)
        for _ in range(5):
            if self.ada_queue and self.ada_queue[0][0] == l:
                self.ada_step(*self.ada_queue.pop(0))
        self.stage("pp_kv")
        self.kv(l, n, True, lambda m: self.Ksb[:, m, a:a + n], ("K",), lambda blk: self.Vsb[:, blk, :], ("V",), a // 128)

    def attention_gen(self, nblk, a, ctx_mode):
        P = self.P
        if ctx_mode:
            kcs = [("c", 0), ("c", 1)]
        else:
            kcs = [("c", 0), ("c", 1), ("l", 0), ("l", 1), ("l", 2)]
        steps = []
        for i in range(nblk):
            for quad in range(4):
                qs = {}
                for ki in range(len(kcs)):
                    steps.append({"i": i, "quad": quad, "ki": ki, "qs": qs})

        def qk(st):
            i, quad, (kind, kk) = st["i"], st["quad"], kcs[st["ki"]]
            ql, uq = 128 * i, a + 128 * i
            m, j = divmod(quad, 2)
            c0 = 8 * m + 4 * j
            bs = P.ps(2, "A")
            st["bs"] = bs
            masked = kind == "l" and kk != 1
            mms = []
            if masked:
                mb = self.maskb[:, 0 if kk == 0 else 1, :]
                for h in range(2):
                    mms.append((self.PS[:, bs + h, :], self.ident[:], mb, True, False))
            for h in range(2):
                if kind == "c":
                    kap = self.Kc[64 * h:64 * h + 64, m, 128 * kk:128 * kk + 128]
                    kkey = ("Kc",)
                else:
                    u0 = uq - 128 + 128 * kk
                    kap = self.Ksb[64 * h:64 * h + 64, m, u0:u0 + 128]
                    kkey = ("K",)
                qap = self.qy[64 * h:64 * h + 64, c0:c0 + 4, ql:ql + 128]
                mms.append((self.PS[:, bs + h, :], kap, qap, not masked, True))
            self.mm(mms, [kkey, ("ident",), ("maskb",)] + [("qy", c0 + t) for t in range(4)],
                    [("ps", bs), ("ps", bs + 1)])

        def ex(st):
            i, (kind, kk) = st["i"], kcs[st["ki"]]
            uq = a + 128 * i
            bs = st["bs"]
            bi = P.bpair()
            st["bi"] = bi
            pt2 = self.Br[:, bi:bi + 2, 0:512]
            rk = [("ps", bs), ("ps", bs + 1)]
            wk = [("B", bi), ("B", bi + 1)]
            if kind == "c":
                self.act(pt2, self.PS[:, bs:bs + 2, :], AF.Exp, rk, wk, scale=0.125)
            else:
                blk = (uq - 128 + 128 * kk) // 128
                self.act(pt2, self.PS[:, bs:bs + 2, :], AF.Exp, rk + [("kbias",)], wk,
                         bias=self.kbias[:, blk:blk + 1], scale=0.125)

        def pv(st):
            i, quad, ki, qs = st["i"], st["quad"], st["ki"], st["qs"]
            kind, kk = kcs[ki]
            uq = a + 128 * i
            m, j = divmod(quad, 2)
            if ki == 0:
                qs["bo"] = P.ps(1, "B")
                qs["bd"] = P.ps(1, "B")
            bo, bd = qs["bo"], qs["bd"]
            first, last = ki == 0, ki == len(kcs) - 1
            bi = st["bi"]
            rd = [("ones64",), ("B", bi), ("B", bi + 1)]
            vaps = []
            for h in range(2):
                g = 2 * m + h
                if kind == "c":
                    vaps.append(self.Vc[:, kk, 64 * g:64 * g + 64])
                    rd.append(("Vc",))
                else:
                    blk = (uq - 128 + 128 * kk) // 128
                    vaps.append(self.Vsb[:, blk, 64 * g:64 * g + 64])
                    rd.append(("V",))
            pts = [self.Br[:, bi + h, 0:512] for h in range(2)]
            mms = [(self.PS[0:64, bo, :], vaps[0], pts[0], first, last),
                   (self.PS[64:128, bo, :], vaps[1], pts[1], first, last),
                   (self.PS[0:64, bd, :], self.ones64[:, :], pts[0], first, last),
                   (self.PS[64:128, bd, :], self.ones64[:, :], pts[1], first, last)]
            self.mm(mms, rd, [("ps", bo), ("ps", bd)])
            if last:
                ql = 128 * i
                c0 = 8 * m + 4 * j
                t1 = P.tmp()
                for t in range(4):
                    self.act(self.Tr[:, t1, 128 * t:128 * t + 128], self.PS[:, bd, 128 * t:128 * t + 128], AF.Ln,
                             [("ps", bd), ("se",)], [("T", t1)], bias=self.sexp[:, c0 + t:c0 + t + 1])
                t2 = P.tmp()
                self.act(self.Tr[:, t2, 0:512], self.Tr[:, t1, 0:512], AF.Exp, [("T", t1)], [("T", t2)], scale=-1.0)
                ov = self.PS[:, bo, :].rearrange("p (c t) -> p c t", t=128)
                self.tt(self.at[:, c0:c0 + 4, ql:ql + 128], ov,
                        self.Tr[:, t2, 0:512].rearrange("p (c t) -> p c t", t=128),
                        ALU.mult, [("ps", bo), ("T", t2)], [("at", c0 + t) for t in range(4)])

        qk(steps[0])
        for idx, st in enumerate(steps):
            if idx + 1 < len(steps):
                qk(steps[idx + 1])
            ex(st)
            pv(st)
            yield

    def conv_gen(self, l, n, st, V):
        P = self.P
        hTk = [("hT", k) for k in range(16)]
        nh, hl = _halves(n + 2)
        N2 = n + 2
        for j in range(16):
            s, ci = st.next()
            bx = P.ps(nh, "C")
            mms = [(self.PS[:, bx + h, 0:hl], self.wsl[s][:, ci * 16 + k, :], self.hT[:, k, h * hl:(h + 1) * hl],
                    k == 0, k == 15) for k in range(16) for h in range(nh)]
            self.mm(mms, [("w", s)] + hTk, self.pskeys(bx, nh))
            t_cx = P.tmp()
            self.tt(self.v3(self.Tr[:, t_cx, 0:N2], nh), self.psv(bx, nh, hl), self.v3(self.valb[:, 0:N2], nh), ALU.mult,
                    self.pskeys(bx, nh) + [("val",)], [("T", t_cx)])
            yield
            s, ci = st.next()
            bc = P.ps(nh, "C")
            mms = [(self.PS[:, bc + h, 0:hl], self.wsl[s][:, ci * 16 + k, :], self.hT[:, k, h * hl:(h + 1) * hl],
                    k == 0, k == 15) for k in range(16) for h in range(nh)]
            self.mm(mms, [("w", s)] + hTk, self.pskeys(bc, nh))
            t_u = P.tmp()
            self.tt(self.v3(self.Tr[:, t_u, 0:N2], nh), self.psv(bc, nh, hl), self.v3(self.Tr[:, t_cx, 0:N2], nh), ALU.mult,
                    self.pskeys(bc, nh) + [("T", t_cx)], [("T", t_u)])
            u = self.Tr[:, t_u, :]
            cw = lambda tap, j=j: V[:, VC_CW + 3 * j + tap:VC_CW + 3 * j + tap + 1]
            t0 = P.tmp()
            P.op("dve", [("T", t_u), ("vecs", l)], [("T", t0)],
                 lambda e, o=self.Tr[:, t0, 0:n], i_=u[:, 1:1 + n], s1=cw(1), s2=V[:, VC_CB + j:VC_CB + j + 1]:
                 e.tensor_scalar(out=o, in0=i_, scalar1=s1, scalar2=s2, op0=ALU.mult, op1=ALU.add))
            ta = P.tmp()
            self.stt(self.Tr[:, ta, 0:n], u[:, 0:n], cw(0), self.Tr[:, t0, 0:n], ALU.mult, ALU.add,
                     [("T", t_u), ("T", t0), ("vecs", l)], [("T", ta)])
            tb = P.tmp()
            self.stt(self.Tr[:, tb, 0:n], u[:, 2:2 + n], cw(2), self.Tr[:, ta, 0:n], ALU.mult, ALU.add,
                     [("T", t_u), ("T", ta), ("vecs", l)], [("T", tb)])
            yield
            s, ci = st.next()
            bb = P.ps(1, "C")
            mms = [(self.PS[:, bb, 0:n], self.wsl[s][:, ci * 16 + k, :], self.hT[:, k, 1:1 + n], k == 0, k == 15)
                   for k in range(16)]
            self.mm(mms, [("w", s)] + hTk, [("ps", bb)])
            self.tt(self.yT[:, j, 0:n], self.PS[:, bb, 0:n], self.Tr[:, tb, 0:n], ALU.mult,
                    [("ps", bb), ("T", tb)], [("y", j)])
            yield

    @staticmethod
    def interleave(ga, na, gb, nb):
        done_b = 0
        for i in range(na):
            next(ga)
            want = ((i + 1) * nb) // na
            while done_b < want:
                next(gb)
                done_b += 1
        for _ in ga:
            pass
        for _ in gb:
            pass

    def tile_pass(self, l, t, src, a, n, dst, dst_a, ctx_mode, kv_only=False, final=False, skip_kv=False):
        P = self.P
        V = self.vecs[l]
        hTk = [("hT", k) for k in range(16)]
        if kv_only:
            self.load_x(src, a, n, 0)
            self.norm(l, t, 0, 1, n)
            self.kv(l, n, False, lambda m: self.Kc[:, m, 0:n], ("Kc",), lambda blk: self.Vc[:, blk, :], ("Vc",), 0)
            return
        self.stage("tile_load")
        self.load_x(src, a, n, 1)
        if not ctx_mode:
            self.load_tabs(a, n, 1)
        else:
            P.op("dve", [], [("val",)], lambda e: e.memset(self.valb[:, 0:n + 2], 1.0))
            P.op("dve", [("val",)], [("val",)], lambda e: e.memset(self.valb[:, 0:1], 0.0))
            P.op("dve", [("val",)], [("val",)], lambda e: e.memset(self.valb[:, n + 1:n + 2], 0.0))
        self.stage("norm1")
        self.norm(l, t, 0, 0, n + 2)
        tag = "L%d%s%d" % (l, "c" if ctx_mode else "m", a)
        K16 = lambda nm: [(nm, c) for c in range(16)]
        self.dump(tag, "hT", self.hT[:, :, 0:n + 2], [128, 16, n + 2], BF16, K16("hT"))
        self.dump(tag, "mod", self.mod[:], [128, 2, 96], F32, [("mod",)])
        self.stage("kvctx")
        if ctx_mode and not skip_kv:
            self.kv(l, n, False, lambda m: self.Kc[:, m, 0:n], ("Kc",), lambda blk: self.Vc[:, blk, :], ("Vc",), 0)
        st = Kern.Stream(self, l, LD_MAIN0)
        self.stage("q")
        pend = None
        for c in range(16):
            s, ci = st.next()
            b = P.ps(1)
            mms = [(self.PS[:, b, 0:n], self.wsl[s][:, ci * 16 + k, :], self.hT[:, k, 1:1 + n], k == 0, k == 15)
                   for k in range(16)]
            if c == 0:
                for k in range(16):
                    self.mm([mms[k]], [("w", s), ("hT", k)], [("ps", b)])
            else:
                self.mm(mms, [("w", s)] + hTk, [("ps", b)])
            if ctx_mode:
                self.act(self.qy[:, c, 0:n], self.PS[:, b, 0:n], AF.Copy, [("ps", b)], [("qy", c)])
            else:
                ra = self.rope_a(b, n)
                if pend is not None:
                    self.rope_b(pend[0], self.qy[:, pend[1], 0:n], [("qy", pend[1])])
                pend = (ra, c)
        if pend is not None:
            self.rope_b(pend[0], self.qy[:, pend[1], 0:n], [("qy", pend[1])])
        self.dump(tag, "q", self.qy[:, :, 0:n], [128, 16, n], BF16, K16("qy"))
        if ctx_mode:
            self.dump(tag, "Kc", self.Kc[:], [128, 2, CTX], BF16, [("Kc",)])
            self.dump(tag, "Vc", self.Vc[:], [128, 2, 256], BF16, [("Vc",)])
        self.stage("attn")
        nblk = n // 128
        na = nblk * 4 * (2 if ctx_mode else 5)
        sv_allow, self.allow_ada = self.allow_ada, False
        self.interleave(self.attention_gen(nblk, a, ctx_mode), na, self.conv_gen(l, n, st, V), 48)
        self.allow_ada = sv_allow
        self.dump(tag, "at", self.at[:, :, 0:n], [128, 16, n], BF16, K16("at"))
        self.dump(tag, "y", self.yT[:, :, 0:n], [128, 16, n], BF16, K16("y"))
        self.stage("merge")
        for j in range(16):
            bs = []
            for which in range(4):
                s, ci = st.next()
                b = P.ps(1)
                src_t, keyn = [(self.hT, "hT"), (self.hT, "hT"), (self.at, "at"), (self.yT, "y")][which]
                off = 1 if which < 2 else 0
                mms = [(self.PS[:, b, 0:n], self.wsl[s][:, ci * 16 + k, :], src_t[:, k, off:off + n], k == 0, k == 15)
                       for k in range(16)]
                self.mm(mms, [("w", s)] + [(keyn, k) for k in range(16)], [("ps", b)])
                bs.append(b)
            bga, bgc, bao, bco = bs
            tg = P.tmp()
            self.act(self.Tr[:, tg, 0:n], self.PS[:, bga, 0:n], AF.Sigmoid, [("ps", bga)], [("T", tg)])
            tc = P.tmp()
            self.act(self.Tr[:, tc, 0:n], self.PS[:, bgc, 0:n], AF.Sigmoid, [("ps", bgc)], [("T", tc)])
            t1 = P.tmp()
            self.tt(self.Tr[:, t1, 0:n], self.PS[:, bao, 0:n], self.Tr[:, tg, 0:n], ALU.mult, [("ps", bao), ("T", tg)], [("T", t1)])
            t2 = P.tmp()
            self.tt(self.Tr[:, t2, 0:n], self.PS[:, bco, 0:n], self.Tr[:, tc, 0:n], ALU.mult, [("ps", bco), ("T", tc)], [("T", t2)])
            self.tt(self.mT[:, j, 0:n], self.Tr[:, t1, 0:n], self.Tr[:, t2, 0:n], ALU.add, [("T", t1), ("T", t2)], [("m", j)])
        self.dump(tag, "m", self.mT[:, :, 0:n], [128, 16, n], BF16, K16("m"))
        self.stage("wo")
        self.ada_need(l, "wo")
        ns2 = self.norm_begin(n, "C")
        self.ring = "six"
        sv_allow, self.allow_ada = self.allow_ada, False
        for j in range(16):
            s, ci = st.next()
            b = P.ps(1, "six")
            mms = [(self.PS[:, b, 0:n], self.wsl[s][:, ci * 16 + k, :], self.mT[:, k, 0:n], k == 0, k == 15)
                   for k in range(16)]
            self.mm(mms, [("w", s)] + [("m", k) for k in range(16)], [("ps", b)])
            xs = self.xt[:, j, 1:1 + n]
            self.stt(xs, self.PS[:, b, 0:n], self.mod[:, t, 32 + j:33 + j], xs, ALU.mult, ALU.add,
                     [("ps", b), ("xt", j), ("mod",)], [("xt", j)])
            if j >= 1:
                self.norm_stat(ns2, j - 1, 1)
        self.norm_stat(ns2, 15, 1)
        self.allow_ada = sv_allow
        self.dump(tag, "xmid", self.xt[:, :, 1:1 + n], [128, 16, n], F32, K16("xt"))
        self.stage("ffn")
        self.ada_need(l, "norm2")
        self.norm_finish(ns2, l, t, 1, 1)
        self.ring = "all"
        for j in range(NFF):
            bs = []
            for which in range(2):
                s, ci = st.next()
                b = P.ps(1)
                mms = [(self.PS[:, b, 0:n], self.wsl[s][:, ci * 16 + k, :], self.hT[:, k, 1:1 + n], k == 0, k == 15)
                       for k in range(16)]
                if j == 0 and which == 0:
                    for k in range(16):
                        self.mm([mms[k]], [("w", s), ("hT", k)], [("ps", b)])
                else:
                    self.mm(mms, [("w", s)] + hTk, [("ps", b)])
                bs.append(b)
            tg = P.tmp()
            self.act(self.Tr[:, tg, 0:n], self.PS[:, bs[0], 0:n], AF.Silu, [("ps", bs[0])], [("T", tg)])
            self.tt(self.actT[:, j, 0:n], self.PS[:, bs[1], 0:n], self.Tr[:, tg, 0:n], ALU.mult,
                    [("ps", bs[1]), ("T", tg)], [("act", j)])
        self.ada_need(l, "ffo")
        for j in range(16):
            s = self.wload_main(l, LD_FO0 + j, NFF * 128)
            b = P.ps(1)
            mms = [(self.PS[:, b, 0:n], self.wsl[s][:, k, :], self.actT[:, k, 0:n], k == 0, k == NFF - 1)
                   for k in range(NFF)]
            self.mm(mms, [("w", s)] + [("act", k) for k in range(NFF)], [("ps", b)])
            xs = self.xt[:, j, 1:1 + n]
            self.stt(xs, self.PS[:, b, 0:n], self.mod[:, t, 80 + j:81 + j], xs, ALU.mult, ALU.add,
                     [("ps", b), ("xt", j), ("mod",)], [("xt", j)])
            if j % 4 == 3 and not final and tag not in self.debug:
                self.store_piece(dst, dst_a, n, j // 4, final)
        self.dump(tag, "xout", self.xt[:, :, 1:1 + n], [128, 16, n], F32, K16("xt"))
        self.stage("store")
        if not final and tag in self.debug:
            for p in range(4):
                self.store_piece(dst, dst_a, n, p, final)
        if final:
            self.norm(l, t, 0, 1, n, final=True)
        if final:
            for p in range(4):
                self.store_piece(dst, dst_a, n, p, final)

    def store_piece(self, dst, dst_a, n, p, final):
        dv = dst.rearrange("(c p) t -> p c t", p=128)
        self.dma("sp", dv[:, 4 * p:4 * p + 4, dst_a:dst_a + n], self.xt[:, 4 * p:4 * p + 4, 1:1 + n],
                 [("xt", c) for c in range(4 * p, 4 * p + 4)],
                 [("x1dram", p)] if not final else [("outdram", p)], self.s_st[p])

    def layer(self, l):
        last = l == 1
        self.allow_ada = False
        self.ada(l)
        csrc = self.ctxT if l == 0 else self.xcT
        sk = self.wload(self.ws[l][LD_K], 4096)
        sv = self.wload(self.ws[l][LD_V], 4096)
        self.kv_slots = (sk, sv)
        self.pinned = {sk, sv}
        self.stage("prepass")
        src = self.xT if l == 0 else self.x1T
        lo, hi = (0, NTOK) if l == 0 else (128, NTOK - 128)
        tiles = [(csrc, 1, CTX, 1, True)]
        a = lo
        while a < hi:
            n = min(512, hi - a)
            tiles.append((src, a, n, 0, False))
            a += n
        self.prepass_all(l, tiles)
        self.kv_slots = None
        self.pinned = set()
        self.stage("main")
        if l == 0 and 1 in self.layers:
            self.ada_queue = self.ada_queue + [(1, i) for i in range(32)]
        lo, hi = (128, NTOK - 128) if l == 0 else (256, NTOK - 256)
        a = lo
        ti = 0
        while a < hi:
            n = min(512, hi - a)
            if last:
                self.tile_pass(l, 0, src, a, n, self.outT, a - 256, False, final=True)
            else:
                self.wb_tile = ti if ti < 4 else None
                self.use_bf16 = ti >= 4
                self.allow_ada = ti in (0, 2, 3)
                self.ada_period = 4 if ti == 0 else 3
                self.tile_pass(l, 0, src, a, n, self.x1T, a, False)
            a += n
            ti += 1
        if not last:
            self.wb_tile = None
            self.use_bf16 = True
            self.allow_ada = False
            self.tile_pass(l, 1, self.ctxT, 1, CTX, self.xcT, 1, True, skip_kv=True)
            self.use_bf16 = False
            self.allow_ada = False


_CACHE = {}


def _get_prog():
    if "k" not in _CACHE:
        _CACHE["k"] = Kern()
    return _CACHE["k"]


def host_inputs(x, c, ctx, c_ctx, ada_w, ada_b, norm1_g, norm2_g, w_in, conv_w, conv_b, sink,
                w_attn_out, w_conv_out, w_o, w_ffn_in, w_ffn_out, final_g, cores=range(8)):
    f = lambda a: np.ascontiguousarray(np.asarray(a, dtype=np.float32))
    x, c, ctx, c_ctx, ada_w, ada_b, norm1_g, norm2_g, w_in, conv_w, conv_b, sink, w_attn_out, w_conv_out, w_o, \
        w_ffn_in, w_ffn_out, final_g = map(f, (x, c, ctx, c_ctx, ada_w, ada_b, norm1_g, norm2_g, w_in, conv_w, conv_b,
                                               sink, w_attn_out, w_conv_out, w_o, w_ffn_in, w_ffn_out, final_g))
    shared = {}
    for l in range(2):
        shared["ws%d" % l] = _build_wstream(l, w_in, w_attn_out, w_conv_out, w_o, w_ffn_in, w_ffn_out)
        shared["wa%d" % l] = _build_wada(l, ada_w)
        shared["vec%d" % l] = _build_vec(l, ada_b, norm1_g, norm2_g, conv_w, conv_b, sink, final_g)
    in_maps = []
    for r in cores:
        b, s = divmod(r, 4)
        base = 2048 * s - 256
        xT = np.zeros((D, NTOK), np.float32)
        lo, hi = max(base, 0), min(base + NTOK, SEQ)
        xT[:, lo - base:hi - base] = x[b, lo:hi, :].T
        ctxT = np.zeros((D, CTX + 2), np.float32)
        ctxT[:, 1:CTX + 1] = ctx[b].T
        cvec = np.zeros((128, 32), np.float32)
        cvec[:, 0::2] = _fm(c[b])
        cvec[:, 1::2] = _fm(c_ctx)
        m = dict(shared)
        m.update({"xT": xT, "ctxT": ctxT, "cvec": cvec, "tab": _build_tab(base)})
        in_maps.append(m)
    return in_maps


def kernel(**inputs):
    in_maps = host_inputs(**inputs)
    k = _get_prog()
    res = run_bass_kernel_spmd(k.nc, in_maps, core_ids=list(range(8)))
    out = np.zeros((2, SEQ, D), np.float32)
    for r in range(8):
        b, s = divmod(r, 4)
        out[b, 2048 * s:2048 * s + 2048, :] = res.results[r]["outT"].T
    return out
```
